# Optimizing a Trainium2 kernel written in Bass

```python
import math
import jax, jax.numpy as jnp
from jax import lax
import numpy as np

D_MODEL = 1024
BATCH = 8
SEQ = 4096
DEPTH = 4

EXPAND = 2
BRANCH = EXPAND * D_MODEL
N_HEADS = 16
QK_HALF = 64
V_DIM = 2 * QK_HALF
QK_WIDTH = N_HEADS * 2 * QK_HALF
N_MIXERS = 2
CHUNK = 128
SGU_GROUPS = 16
SGU_GROUP_DIM = BRANCH // SGU_GROUPS
N_BUCKETS = 32
MAX_DISTANCE = 128
Q_BLOCK = 128
EPS = 1e-6
N_ATTN = (DEPTH + 1) // 2
N_SGU = DEPTH // 2

kernel_name = "hybrid_diffattn_chunked_sgu"


def rms_norm(x, g):
    xf = x.astype(jnp.float32)
    y = xf * lax.rsqrt(jnp.mean(xf * xf, axis=-1, keepdims=True) + EPS)
    return (y * g.astype(jnp.float32)).astype(x.dtype)


def t5_bucket(rel):
    n = jnp.maximum(-rel, 0)
    max_exact = N_BUCKETS // 2
    nf = jnp.maximum(n, 1).astype(jnp.float32)
    large = max_exact + (jnp.log(nf / max_exact) / math.log(MAX_DISTANCE / max_exact)
                         * (N_BUCKETS - max_exact)).astype(jnp.int32)
    large = jnp.minimum(large, N_BUCKETS - 1)
    return jnp.where(n < max_exact, n, large)


def diff_attention_branch(h, w_in, lq1, lk1, lq2, lk2, subln_g, rel_bias, lam_init):
    B, S, _ = h.shape
    proj = h @ w_in
    q = proj[..., :QK_WIDTH].reshape(B, S, N_HEADS, 2, QK_HALF)
    k = proj[..., QK_WIDTH:2 * QK_WIDTH].reshape(B, S, N_HEADS, 2, QK_HALF)
    v = proj[..., 2 * QK_WIDTH:2 * QK_WIDTH + BRANCH].reshape(B, S, N_HEADS, V_DIM)
    gate = proj[..., 2 * QK_WIDTH + BRANCH:]
    lam = (jnp.exp(jnp.sum(lq1.astype(jnp.float32) * lk1.astype(jnp.float32)))
           - jnp.exp(jnp.sum(lq2.astype(jnp.float32) * lk2.astype(jnp.float32))) + lam_init)
    q = q.transpose(3, 0, 2, 1, 4)
    k = k.transpose(3, 0, 2, 1, 4)
    v = v.transpose(0, 2, 1, 3)
    k1, k2 = k[0], k[1]
    nb = S // Q_BLOCK
    qb = q.reshape(2, B, N_HEADS, nb, Q_BLOCK, QK_HALF).transpose(3, 0, 1, 2, 4, 5)
    k_pos = jnp.arange(S, dtype=jnp.int32)
    scale = QK_HALF ** -0.5
    table = rel_bias.astype(jnp.float32)

    def block(args):
        i, q_i = args
        q_pos = i * Q_BLOCK + jnp.arange(Q_BLOCK, dtype=jnp.int32)
        rel = k_pos[None, :] - q_pos[:, None]
        bias = table[t5_bucket(rel)].transpose(2, 0, 1)
        causal = rel <= 0

        def probs(qq, kk):
            s = jnp.einsum('bhqd,bhkd->bhqk', qq, kk).astype(jnp.float32) * scale + bias
            s = jnp.where(causal, s, -jnp.inf)
            return jax.nn.softmax(s, axis=-1)

        a = probs(q_i[0], k1) - lam * probs(q_i[1], k2)
        return jnp.einsum('bhqk,bhkd->bhqd', a.astype(v.dtype), v)

    o = lax.map(block, (jnp.arange(nb, dtype=jnp.int32), qb))
    o = o.transpose(1, 0, 3, 2, 4).reshape(B, S, N_HEADS, V_DIM)
    o = rms_norm(o, subln_g) * (1.0 - lam_init)
    o = o.reshape(B, S, BRANCH)
    return o * jax.nn.silu(gate)


def spatial_gating_branch(h, w_in, v_gain, w_s, b_s):
    B, S, _ = h.shape
    proj = h @ w_in
    u = proj[..., :BRANCH]
    v = rms_norm(proj[..., BRANCH:2 * BRANCH], v_gain)
    gate = proj[..., 2 * BRANCH:]
    v = v.reshape(B, S // CHUNK, CHUNK, SGU_GROUPS, SGU_GROUP_DIM)
    causal = jnp.tril(jnp.ones((CHUNK, CHUNK), dtype=bool))
    w = jnp.where(causal[None], w_s, jnp.zeros((), w_s.dtype))
    y = jnp.einsum('gts,bcsgd->bctgd', w, v) + b_s.T[:, :, None]
    y = y.reshape(B, S, BRANCH)
    return u * y * jax.nn.silu(gate)


def setup_inputs(seed: int = 0) -> dict:
    key = jax.random.key(seed)
    ks = jax.random.split(key, 17)
    f32 = jnp.float32
    n = lambda k, shape, s: jax.random.normal(k, shape, f32) * s
    return {
        "x": n(ks[0], (BATCH, SEQ, D_MODEL), 1.0),
        "rel_bias": n(ks[1], (N_BUCKETS, N_HEADS), 0.5),
        "attn_norm": 1.0 + n(ks[2], (N_ATTN, D_MODEL), 0.02),
        "attn_w_in": n(ks[3], (N_ATTN, D_MODEL, 2 * QK_WIDTH + 2 * BRANCH), D_MODEL ** -0.5),
        "attn_lam_q1": n(ks[4], (N_ATTN, QK_HALF), 0.1),
        "attn_lam_k1": n(ks[5], (N_ATTN, QK_HALF), 0.1),
        "attn_lam_q2": n(ks[6], (N_ATTN, QK_HALF), 0.1),
        "attn_lam_k2": n(ks[7], (N_ATTN, QK_HALF), 0.1),
        "attn_subln": 1.0 + n(ks[8], (N_ATTN, V_DIM), 0.02),
        "attn_w_out": n(ks[9], (N_ATTN, BRANCH, D_MODEL), BRANCH ** -0.5),
        "sgu_norm": 1.0 + n(ks[10], (N_SGU, D_MODEL), 0.02),
        "sgu_w_in": n(ks[11], (N_SGU, D_MODEL, 3 * BRANCH), D_MODEL ** -0.5),
        "sgu_v_norm": 1.0 + n(ks[12], (N_SGU, BRANCH), 0.02),
        "sgu_w_s": n(ks[13], (N_SGU, SGU_GROUPS, CHUNK, CHUNK), CHUNK ** -0.5),
        "sgu_b_s": 1.0 + n(ks[14], (N_SGU, SGU_GROUPS, CHUNK), 0.02),
        "sgu_w_out": n(ks[15], (N_SGU, BRANCH, D_MODEL), BRANCH ** -0.5),
        "final_norm": 1.0 + n(ks[16], (D_MODEL,), 0.02),
    }


def reference(x, rel_bias, attn_norm, attn_w_in, attn_lam_q1, attn_lam_k1, attn_lam_q2,
              attn_lam_k2, attn_subln, attn_w_out, sgu_norm, sgu_w_in, sgu_v_norm, sgu_w_s,
              sgu_b_s, sgu_w_out, final_norm):
    for i in range(DEPTH):
        j = i // N_MIXERS
        if i % N_MIXERS == 0:
            lam_init = 0.8 - 0.6 * math.exp(-0.3 * i)
            h = rms_norm(x, attn_norm[j])
            y = diff_attention_branch(h, attn_w_in[j], attn_lam_q1[j], attn_lam_k1[j],
                                      attn_lam_q2[j], attn_lam_k2[j], attn_subln[j],
                                      rel_bias, lam_init)
            x = x + y @ attn_w_out[j]
        else:
            h = rms_norm(x, sgu_norm[j])
            y = spatial_gating_branch(h, sgu_w_in[j], sgu_v_norm[j], sgu_w_s[j], sgu_b_s[j])
            x = x + y @ sgu_w_out[j]
    return rms_norm(x, final_norm)
```

```python
import math
import os
from contextlib import ExitStack

import numpy as np
import concourse.bass as bass
import concourse.mybir as mybir
from concourse.bass_utils import run_bass_kernel_spmd

F32 = mybir.dt.float32
BF16 = mybir.dt.bfloat16
AF = mybir.ActivationFunctionType
ALU = mybir.AluOpType

ENGS = ["tensor", "vector", "scalar", "gpsimd", "sync"]
NDS = 12
S = 4096
D = 1024
NH = 16
EPS = 1e-6
MASKV = -30000.0
DEPTH = 4


class R:
    __slots__ = ("wr", "rd")

    def __init__(self):
        self.wr = None
        self.rd = {}


def _mkwait(sem, v):
    return lambda e: e.wait_ge(sem, v)


def _mkmarked(fn, sem):
    return lambda e: fn(e).then_inc(sem, 1)


def _mkdma(out, in_, sem, kw):
    return lambda e: e.dma_start(out=out, in_=in_, **kw).then_inc(sem, 16)


class Prog:
    def __init__(self, nc, es):
        self.nc = nc
        self.q = {e: [] for e in ENGS}
        self.sem = {e: es.enter_context(nc.semaphore("s_" + e)) for e in ENGS}
        self.cnt = {e: 0 for e in ENGS}
        self.seen = {e: {} for e in ENGS}
        self.dsem = {}
        self.dpool = {}
        for e in ("sync", "gpsimd"):
            keys = []
            for i in range(NDS):
                k = "D%s%d" % (e, i)
                self.dsem[k] = es.enter_context(nc.semaphore(k))
                keys.append(k)
            self.dpool[e] = [keys, [0] * NDS, 0]

    def _waits(self, eng, toks):
        for (key, v) in toks:
            if key == eng and eng == "tensor":
                continue
            if self.seen[eng].get(key, 0) >= v:
                continue
            self.seen[eng][key] = v
            sem = self.sem[key] if key in self.sem else self.dsem[key]
            self.q[eng].append(_mkwait(sem, v))

    @staticmethod
    def _deps(reads, writes):
        toks = []
        for r in reads:
            if r.wr is not None:
                toks.append(r.wr)
        for r in writes:
            if r.wr is not None:
                toks.append(r.wr)
            toks.extend(r.rd.items())
        return toks

    @staticmethod
    def _update(tok, reads, writes):
        k, v = tok
        for r in reads:
            if r.rd.get(k, 0) < v:
                r.rd[k] = v
        for r in writes:
            r.wr = tok
            r.rd = {}

    def op(self, eng, fn, reads=(), writes=(), mark=True):
        self._waits(eng, self._deps(reads, writes))
        if mark:
            self.cnt[eng] += 1
            tok = (eng, self.cnt[eng])
            self.q[eng].append(_mkmarked(fn, self.sem[eng]))
        else:
            assert eng == "tensor"
            tok = (eng, self.cnt[eng] + 1)
            self.q[eng].append(fn)
        self._update(tok, reads, writes)
        return tok

    def dma(self, eng, out, in_, reads=(), writes=(), **kw):
        keys, cnts, nxt = self.dpool[eng]
        i = nxt % NDS
        self.dpool[eng][2] = nxt + 1
        key = keys[i]
        toks = self._deps(reads, writes)
        if cnts[i] > 0:
            toks.append((key, cnts[i]))
        self._waits(eng, toks)
        cnts[i] += 16
        tok = (key, cnts[i])
        self.q[eng].append(_mkdma(out, in_, self.dsem[key], kw))
        self._update(tok, reads, writes)
        return tok

    def drain(self):
        toks = []
        for e in self.dpool:
            keys, cnts, _ = self.dpool[e]
            for k, c in zip(keys, cnts):
                if c > 0:
                    toks.append((k, c))
        self._waits("sync", toks)

    def flush(self):
        nc = self.nc
        with nc.Block() as block:
            for e in ENGS:
                fns = self.q[e]

                def body(eng, fns=fns):
                    for f in fns:
                        f(eng)

                getattr(block, e)(body)
        self.q = {e: [] for e in ENGS}


def mm(P, out, lhsT, rhs, start, stop, reads, writes, mark):
    return P.op("tensor", lambda e: e.matmul(out, lhsT, rhs, start=start, stop=stop),
                reads=reads, writes=writes, mark=mark)


def act(P, out, in_, func, reads, writes, scale=None, bias=None, accum_out=None):
    kw = {}
    if scale is not None:
        kw["scale"] = scale
    if bias is not None:
        kw["bias"] = bias
    if accum_out is not None:
        kw["accum_out"] = accum_out
    return P.op("scalar", lambda e: e.activation(out=out, in_=in_, func=func, **kw), reads=reads, writes=writes)


def tt(P, eng, out, in0, in1, op, reads, writes):
    return P.op(eng, lambda e: e.tensor_tensor(out=out, in0=in0, in1=in1, op=op), reads=reads, writes=writes)


def ts(P, eng, out, in0, s1, s2, op0, op1, reads, writes):
    if op1 is None:
        return P.op(eng, lambda e: e.tensor_scalar(out=out, in0=in0, scalar1=s1, scalar2=None, op0=op0),
                    reads=reads, writes=writes)
    return P.op(eng, lambda e: e.tensor_scalar(out=out, in0=in0, scalar1=s1, scalar2=s2, op0=op0, op1=op1),
                reads=reads, writes=writes)


def stt(P, out, in0, scalar, in1, op0, op1, reads, writes):
    return P.op("vector", lambda e: e.scalar_tensor_tensor(out=out, in0=in0, scalar=scalar, in1=in1, op0=op0, op1=op1),
                reads=reads, writes=writes)


def cp(P, eng, out, in_, reads, writes):
    return P.op(eng, lambda e: e.tensor_copy(out=out, in_=in_), reads=reads, writes=writes)


def build_program(nlayers=DEPTH):
    nc = bass.Bass("TRN2", target_bir_lowering=False)

    def din(name, shape, dt=F32):
        return nc.dram_tensor(name, list(shape), dt, kind="ExternalInput").ap()

    xT = din("xT", [D, S])
    bmg = din("bmg", [128, NH * 256])
    rb31 = din("rb31", [128, NH])
    maskc = din("maskc", [128, 256])
    tril = din("tril", [128, 128])
    a_norm = din("a_norm", [2, 128, 8])
    a_win = din("a_win", [2, NH, 128, 8, 512])
    a_lam = din("a_lam", [2, 128, 256])
    a_subln = din("a_subln", [2, 128, 1])
    a_wout = din("a_wout", [2, 128, 8, 16, 128])
    s_norm = din("s_norm", [2, 128, 8])
    s_win = din("s_win", [2, 128, 8, 6144])
    s_vg = din("s_vg", [2, 128, 2048])
    s_ws = din("s_ws", [2, 128, 16, 128])
    s_bs = din("s_bs", [2, 128, 2048])
    s_wout = din("s_wout", [2, 128, 16, 1024])
    f_norm = din("f_norm", [128, 8])
    outT = nc.dram_tensor("outT", [D, S], F32, kind="ExternalOutput").ap()
    xres = nc.dram_tensor("xres", [D, S], F32, kind="Internal").ap()
    yT = nc.dram_tensor("yT", [2048, S], BF16, kind="Internal").ap()

    xT_v = xT.rearrange("(c p) t -> p c t", p=128)
    xres_v = xres.rearrange("(c p) t -> p c t", p=128)
    outT_v = outT.rearrange("(c p) t -> p c t", p=128)
    yT_v = yT.rearrange("(k p) t -> p k t", p=128)

    xres_R = [R() for _ in range(16)]
    yT_R = [R() for _ in range(8)]
    out_R = [R() for _ in range(16)]

    with ExitStack() as es:
        P = Prog(nc, es)
        ps = es.enter_context(nc.psum_tensor("ps", [128, 8, 512], F32))
        bankR = [R() for _ in range(8)]
        ones_f = es.enter_context(nc.sbuf_tensor("ones_f", [128, 128], F32))
        ones_b = es.enter_context(nc.sbuf_tensor("ones_b", [128, 128], BF16))
        fnorm_t = es.enter_context(nc.sbuf_tensor("fnorm_t", [128, 8], F32))
        constR = R()
        P.op("vector", lambda e: e.memset(ones_f[:], 1.0), writes=[constR])
        P.op("vector", lambda e: e.memset(ones_b[:], 1.0), writes=[constR])
        P.dma("sync", fnorm_t[:], f_norm, writes=[constR])

        rot = {"g": 0}
        li_box = [0]
        pre = {}

        def next_bank(lo=0, hi=8):
            b = lo + rot["g"] % (hi - lo)
            rot["g"] += 1
            return b

        def norm_fm(xt, xt_R, n, gain, gain_R, out_fn, out_R_, sq, sq_R, lnb, rstd, tmpR, banks):
            b = next_bank(*banks)
            for c in range(8):
                k = c % 2
                act(P, sq[k][:, 0:n], xt[:, c, :], AF.Square, reads=[xt_R], writes=[sq_R[k]], scale=1.0 / 32.0)
                mm(P, ps[:, b, 0:n], ones_f[:], sq[k][:, 0:n], c == 0, c == 7,
                   reads=[sq_R[k], constR], writes=[bankR[b]], mark=True)
            act(P, lnb[:, 0:n], ps[:, b, 0:n], AF.Ln, reads=[bankR[b]], writes=[tmpR[0]], scale=1.0, bias=EPS)
            act(P, rstd[:, 0:n], lnb[:, 0:n], AF.Exp, reads=[tmpR[0]], writes=[tmpR[1]], scale=-0.5)
            for c in range(8):
                stt(P, out_fn(c), xt[:, c, :], gain[:, c:c + 1], rstd[:, 0:n], ALU.mult, ALU.mult,
                    reads=[xt_R, gain_R, tmpR[1]], writes=[out_R_])

        stop = os.environ.get("K_STOP", "")

        def attn_layer(j, src_v, src_is_x):
            lam_init = 0.8 - 0.6 * math.exp(-0.3 * (2 * j))
            with ExitStack() as les:
                sb = lambda name, shape, dt: les.enter_context(nc.sbuf_tensor(name + "_L%d" % li_box[0], shape, dt))
                hT = sb("hT", [128, 8, S], BF16)
                hT_R = [R() for _ in range(8)]
                with ExitStack() as pes:
                    sba = lambda name, shape, dt: pes.enter_context(nc.sbuf_tensor(name + "_L%d" % li_box[0], shape, dt))
                    xt = [sba("xtA%d" % i, [128, 8, 512], F32) for i in range(2)]
                    xt_R = [R(), R()]
                    gain = sba("gainA", [128, 8], F32)
                    gain_R = R()
                    sq = [sba("sqA%d" % i, [128, 512], F32) for i in range(2)]
                    sq_R = [R(), R()]
                    lnb = sba("lnbA", [128, 512], F32)
                    rstd = sba("rstdA", [128, 512], F32)
                    tmpR = [R(), R()]
                    P.dma("sync", gain[:], a_norm[j], writes=[gain_R])
                    for tb in range(8):
                        k = tb % 2
                        rd = [] if src_is_x else [xres_R[2 * tb], xres_R[2 * tb + 1]]
                        P.dma("sync", xt[k][:], src_v[:, :, tb * 512:(tb + 1) * 512], reads=rd, writes=[xt_R[k]])
                        norm_fm(xt[k], xt_R[k], 512, gain, gain_R,
                                lambda c, tb=tb: hT[:, c, tb * 512:(tb + 1) * 512], hT_R[tb],
                                sq, sq_R, lnb, rstd, tmpR, (0, 8))
                    P.drain()
                    P.flush()
                if stop == "A":
                    return
                with ExitStack() as pes:
                    sbb = lambda name, shape, dt: pes.enter_context(nc.sbuf_tensor(name + "_L%d" % li_box[0], shape, dt))
                    wh = [sbb("wh%d" % i, [128, 8, 512], BF16) for i in range(2)]
                    wh_R = [R(), R()]
                    QT = sbb("QT", [128, S], BF16)
                    KT = [sbb("KT%d" % i, [128, S], BF16) for i in range(2)]
                    kz_R = R()
                    P.op("gpsimd", lambda e: e.memset(KT[0][64:128, :], 0.0), writes=[kz_R])
                    P.op("gpsimd", lambda e: e.memset(KT[1][0:64, :], 0.0), writes=[kz_R])
                    Vt = sbb("Vt", [128, 32, 128], BF16)
                    GT = sbb("GT", [128, S], F32)
                    QT_R, KT_R, V_R, GT_R = R(), R(), R(), R()
                    BM = sbb("BM", [128, NH, 256], F32)
                    BM_R = R()
                    rb = sbb("rb", [128, NH], F32)
                    mk = sbb("mk", [128, 256], F32)
                    lamt = sbb("lamt", [128, 256], F32)
                    lamp = sbb("lamp", [128, 128], F32)
                    lams = sbb("lams", [128, 8], F32)
                    gsub = sbb("gsub", [128, 2], F32)
                    lam_R = R()
                    Pb = [sbb("Pb%d" % i, [128, 2, 512], BF16) for i in range(4)]
                    Pb_R = [[R(), R()] for _ in range(4)]
                    evq = []
                    r12 = sbb("r12", [128, 2, 512], F32)
                    Lsb = [sbb("Lsb%d" % i, [128, 2, 512], F32) for i in range(2)]
                    O1s = [sbb("O1s%d" % i, [128, 512], F32) for i in range(2)]
                    O2s = [sbb("O2s%d" % i, [128, 512], F32) for i in range(2)]
                    Lsb_R, O1s_R, O2s_R = [R(), R()], [R(), R()], [R(), R()]
                    ea = sbb("ea", [128, 512], F32)
                    eb = sbb("eb", [128, 512], F32)
                    ed = sbb("ed", [128, 512], F32)
                    esq = sbb("esq", [128, 512], F32)
                    eln = sbb("eln", [128, 512], F32)
                    ers = sbb("ers", [128, 512], F32)
                    et = sbb("et", [128, 512], F32)
                    yo = [sbb("yo%d" % i, [128, 512], BF16) for i in range(2)]
                    r12_R, ea_R, eb_R, ed_R, esq_R, eln_R, ers_R, et_R = [R() for _ in range(8)]
                    yo_R = [R(), R()]

                    P.dma("sync", BM[:].rearrange("p h c -> p (h c)"), bmg, writes=[BM_R])
                    P.dma("sync", rb[:], rb31, writes=[lam_R])
                    P.dma("sync", mk[:], maskc, writes=[lam_R])
                    P.dma("sync", lamt[:], a_lam[j], writes=[lam_R])
                    P.dma("sync", gsub[:, 0:1], a_subln[j], writes=[lam_R])
                    for h in range(NH):
                        stt(P, BM[:, h, :], BM[:, h, :], rb[:, h:h + 1], mk[:], ALU.subtract, ALU.add,
                            reads=[lam_R], writes=[BM_R])
                    tt(P, "vector", lamp[:, 0:64], lamt[:, 0:64], lamt[:, 64:128], ALU.mult, reads=[lam_R], writes=[lam_R])
                    tt(P, "vector", lamp[:, 64:128], lamt[:, 128:192], lamt[:, 192:256], ALU.mult, reads=[lam_R], writes=[lam_R])
                    P.op("vector", lambda e: e.reduce_sum(out=lams[:, 0:1], in_=lamp[:, 0:64], axis=mybir.AxisListType.X),
                         reads=[lam_R], writes=[lam_R])
                    P.op("vector", lambda e: e.reduce_sum(out=lams[:, 1:2], in_=lamp[:, 64:128], axis=mybir.AxisListType.X),
                         reads=[lam_R], writes=[lam_R])
                    act(P, lams[:, 2:4], lams[:, 0:2], AF.Exp, reads=[lam_R], writes=[lam_R])
                    tt(P, "vector", lams[:, 4:5], lams[:, 3:4], lams[:, 2:3], ALU.subtract, reads=[lam_R], writes=[lam_R])
                    ts(P, "vector", lams[:, 5:6], lams[:, 4:5], -lam_init, None, ALU.add, None, reads=[lam_R], writes=[lam_R])
                    ts(P, "vector", gsub[:, 1:2], gsub[:, 0:1], 1.0 - lam_init, None, ALU.mult, None, reads=[lam_R], writes=[lam_R])
                    neg_lam = lams[:, 5:6]
                    gs = gsub[:, 1:2]

                    if stop != "B0a":
                        P.dma("gpsimd", wh[0][:], a_win[j, 0], writes=[wh_R[0]], max_dma_last_dim=2048)
                    pbi_box, sp_box, ev_box, tick_box = [0], [0], [0], [0]
                    pending = []

                    def tick():
                        tick_box[0] += 1
                        while pending and pending[0][0] <= tick_box[0]:
                            _, h_, qb_ = pending.pop(0)
                            epilogue2(h_, qb_)

                    for h in range(NH):
                        if stop in ("B0a", "B0b"):
                            break
                        w = wh[h % 2]
                        wR = wh_R[h % 2]
                        if h + 1 < NH:
                            P.dma("gpsimd", wh[(h + 1) % 2][:], a_win[j, h + 1], writes=[wh_R[(h + 1) % 2]], max_dma_last_dim=2048)
                        for tb in range(8):
                            if stop.startswith("B0c") and tb >= int(stop[3:] or 1):
                                break
                            tsl = slice(tb * 512, (tb + 1) * 512)
                            for part in (0, 1, 3):
                                if stop == "B0d" and part != 0:
                                    continue
                                b = next_bank(0, 4)
                                for c in range(8):
                                    mm(P, ps[:, b, :], w[:, c, part * 128:(part + 1) * 128], hT[:, c, tsl],
                                       c == 0, c == 7, reads=[wR, hT_R[tb]], writes=[bankR[b]], mark=(c == 7))
                                if part == 0:
                                    act(P, QT[:, tsl], ps[:, b, :], AF.Copy, reads=[bankR[b]], writes=[QT_R], scale=0.125)
                                elif part == 1:
                                    cp(P, "vector", KT[0][0:64, tsl], ps[0:64, b, :], reads=[bankR[b]], writes=[KT_R])
                                    cp(P, "vector", KT[1][64:128, tsl], ps[64:128, b, :], reads=[bankR[b]], writes=[KT_R])
                                else:
                                    act(P, GT[:, tsl], ps[:, b, :], AF.Silu, reads=[bankR[b]], writes=[GT_R])
                                if h > 0:
                                    tick()
                            b = next_bank(0, 4)
                            for t4 in range(4):
                                tt_ = tb * 4 + t4
                                for c in range(8):
                                    mm(P, ps[:, b, t4 * 128:(t4 + 1) * 128], hT[:, c, tt_ * 128:(tt_ + 1) * 128],
                                       w[:, c, 256:384], c == 0, c == 7, reads=[wR, hT_R[tb]], writes=[bankR[b]],
                                       mark=(c == 7 and t4 == 3))
                            cp(P, "vector", Vt[:, tb * 4:(tb + 1) * 4, :].rearrange("p a b -> p (a b)"), ps[:, b, :],
                               reads=[bankR[b]], writes=[V_R])
                        if stop.startswith("B0"):
                            break
                        nqb = int(stop[1:]) if stop.startswith("Q") else 8
                        its = []
                        for qb_ in range(nqb):
                            far = list(range(max(0, 4 * qb_ - 1)))
                            near = list(range(max(0, 4 * qb_ - 1), 4 * (qb_ + 1)))
                            order = []
                            if not far:
                                order = near
                            else:
                                nf, nn = len(far), len(near)
                                fi = ni = 0
                                while fi < nf or ni < nn:
                                    if fi < nf and (ni >= nn or fi * nn <= ni * nf):
                                        order.append(far[fi]); fi += 1
                                    else:
                                        order.append(near[ni]); ni += 1
                            for pos, kt_ in enumerate(order):
                                its.append((qb_, kt_, pos == 0, pos == len(order) - 1))

                        def emit_S(qb, kt):
                            jj = kt - 4 * qb
                            c0 = max(0, 128 * jj)
                            sb0 = 2 * (sp_box[0] % 2)
                            sp_box[0] += 1
                            q0 = qb * 512
                            ksl = slice(kt * 128, (kt + 1) * 128)
                            mm(P, ps[:, sb0, c0:512], KT[0][:, ksl], QT[:, q0 + c0:q0 + 512], True, True,
                               reads=[KT_R, QT_R, kz_R], writes=[bankR[sb0]], mark=False)
                            mm(P, ps[:, sb0 + 1, c0:512], KT[1][:, ksl], QT[:, q0 + c0:q0 + 512], True, True,
                               reads=[KT_R, QT_R, kz_R], writes=[bankR[sb0 + 1]], mark=True)
                            return (sb0, c0, jj)

                        def epilogue1(h, qb):
                            k2 = ev_box[0] % 2
                            ev_box[0] += 1
                            cp(P, "vector", Lsb[k2][:], ps[:, 6:8, :], reads=[bankR[6], bankR[7]], writes=[Lsb_R[k2]])
                            act(P, O1s[k2][:], ps[:, 4, :], AF.Copy, reads=[bankR[4]], writes=[O1s_R[k2]])
                            cp(P, "vector", O2s[k2][:], ps[:, 5, :], reads=[bankR[5]], writes=[O2s_R[k2]])
                            P.op("vector", lambda e: e.reciprocal(out=r12[:], in_=Lsb[k2][:]),
                                 reads=[Lsb_R[k2]], writes=[r12_R])
                            tt(P, "gpsimd", ea[:], O1s[k2][:], r12[:, 0, :], ALU.mult, reads=[O1s_R[k2], r12_R], writes=[ea_R])
                            tt(P, "gpsimd", eb[:], O2s[k2][:], r12[:, 1, :], ALU.mult, reads=[O2s_R[k2], r12_R], writes=[eb_R])
                            stt(P, ed[:], eb[:], neg_lam, ea[:], ALU.mult, ALU.add, reads=[ea_R, eb_R, lam_R], writes=[ed_R])
                            tt(P, "gpsimd", esq[:], ed[:], ed[:], ALU.mult, reads=[ed_R], writes=[esq_R])

                        def epilogue2(h, qb):
                            q0 = qb * 512
                            sbk = 2 * (sp_box[0] % 2)
                            mm(P, ps[:, sbk, :], ones_f[:], esq[:], True, True, reads=[constR, esq_R],
                               writes=[bankR[sbk], bankR[sbk + 1]], mark=True)
                            act(P, eln[:], ps[:, sbk, :], AF.Ln, reads=[bankR[sbk], bankR[sbk + 1]], writes=[eln_R],
                                scale=1.0 / 128.0, bias=EPS)
                            act(P, ers[:], eln[:], AF.Exp, reads=[eln_R], writes=[ers_R], scale=-0.5)
                            stt(P, et[:], ed[:], gs, ers[:], ALU.mult, ALU.mult, reads=[ed_R, ers_R, lam_R], writes=[et_R])
                            yk = (h * 8 + qb) % 2
                            tt(P, "gpsimd", yo[yk][:], et[:], GT[:, q0:q0 + 512], ALU.mult, reads=[et_R, GT_R],
                               writes=[yo_R[yk]])
                            P.dma("sync", yT_v[:, h, q0:q0 + 512], yo[yk][:], reads=[yo_R[yk]], writes=[yT_R[qb]])

                        st = emit_S(*its[0][:2])
                        for i, (qb, kt, first, last) in enumerate(its):
                            nkt = 4 * (qb + 1)
                            sb0, c0, jj = st
                            if i + 1 < len(its):
                                st = emit_S(*its[i + 1][:2])
                            pb = Pb[pbi_box[0] % 4]
                            pR = Pb_R[pbi_box[0] % 4]
                            pbi_box[0] += 1
                            for m in range(2):
                                if jj >= -1:
                                    if jj == -1:
                                        a0, wd, bo = 0, 128, 128
                                    else:
                                        a0, wd, bo = 128 * jj, min(256, 512 - 128 * jj), 0
                                    pv = ps[:, sb0 + m, a0:a0 + wd]
                                    tt(P, "vector", pv, pv, BM[:, h, bo:bo + wd], ALU.add, reads=[BM_R],
                                       writes=[bankR[sb0 + m]])
                                act(P, pb[:, m, c0:512], ps[:, sb0 + m, c0:512], AF.Exp,
                                    reads=[bankR[sb0 + m]], writes=[pR[m]])
                            if evq:
                                epilogue1(*evq.pop(0))
                            for m in range(2):
                                mm(P, ps[:, 4 + m, c0:512], Vt[:, kt, :], pb[:, m, c0:512], first, last,
                                   reads=[V_R, pR[m]], writes=[bankR[4 + m]], mark=False)
                                mm(P, ps[:, 6 + m, c0:512], ones_b[:], pb[:, m, c0:512], first, last,
                                   reads=[constR, pR[m]], writes=[bankR[6 + m]], mark=True)
                            tick()
                            if last:
                                evq.append((h, qb))
                                if i + 1 == len(its):
                                    epilogue1(*evq.pop(0))
                                pending.append((tick_box[0] + (6 if qb == 0 else 12), h, qb))
                        if h == NH - 1 or stop:
                            while pending:
                                _, h_, qb_ = pending.pop(0)
                                epilogue2(h_, qb_)
                        if stop == "B1" or stop.startswith("Q"):
                            break
                    P.drain()
                    P.flush()
                if stop:
                    return
            if li_box[0] + 1 < nlayers:
                wes = ExitStack()
                wi_n = wes.enter_context(nc.sbuf_tensor("wiS_L%d" % (li_box[0] + 1), [128, 8, 6144], BF16))
                pre["wi"], pre["wi_R"], pre["es"] = wi_n, R(), wes
            with ExitStack() as pes:
                sbc = lambda name, shape, dt: pes.enter_context(nc.sbuf_tensor(name + "_L%d" % li_box[0], shape, dt))
                wo = sbc("woC", [128, 8, 16, 128], BF16)
                wo_R = [R() for _ in range(8)]
                yt = [sbc("ytC%d" % i, [128, 16, 512], BF16) for i in range(2)]
                yt_R = [R(), R()]
                xt = [sbc("xtC%d" % i, [128, 8, 512], F32) for i in range(2)]
                xt_R = [R(), R()]
                for c in range(8):
                    P.dma("gpsimd", wo[:, c].rearrange("p k m -> p (k m)"), a_wout[j, :, c].rearrange("p k m -> p (k m)"),
                          writes=[wo_R[c]], max_dma_last_dim=2048)
                if "wi" in pre:
                    for c in range(8):
                        for q3 in range(3):
                            P.dma("gpsimd", pre["wi"][:, c, q3 * 2048:(q3 + 1) * 2048],
                                  s_win[j, :, c, q3 * 2048:(q3 + 1) * 2048], writes=[pre["wi_R"]], max_dma_last_dim=2048)

                def c_load(tb):
                    k = tb % 2
                    tsl = slice(tb * 512, (tb + 1) * 512)
                    rd = [] if src_is_x else [xres_R[2 * tb], xres_R[2 * tb + 1]]
                    P.dma("sync", yt[k][:], yT_v[:, :, tsl], reads=[yT_R[tb]], writes=[yt_R[k]])
                    P.dma("sync", xt[k][:], src_v[:, :, tsl], reads=rd, writes=[xt_R[k]])

                c_load(0)
                for tb in range(8):
                    k = tb % 2
                    tsl = slice(tb * 512, (tb + 1) * 512)
                    if tb + 1 < 8:
                        c_load(tb + 1)
                    for c in range(8):
                        b = next_bank(0, 8)
                        for kc in range(16):
                            mm(P, ps[:, b, :], wo[:, c, kc, :], yt[k][:, kc, :], kc == 0, kc == 15,
                               reads=[wo_R[c], yt_R[k]], writes=[bankR[b]], mark=(kc == 15))
                        tt(P, "vector", xt[k][:, c, :], ps[:, b, :], xt[k][:, c, :], ALU.add,
                           reads=[bankR[b]], writes=[xt_R[k]])
                    P.dma("sync", xres_v[:, :, tsl], xt[k][:], reads=[xt_R[k]],
                          writes=[xres_R[2 * tb], xres_R[2 * tb + 1]])
                P.drain()
                P.flush()

        def sgu_layer(j, final):
            NT = 256
            with ExitStack() as pes:
                sb = lambda name, shape, dt: pes.enter_context(nc.sbuf_tensor(name + "_L%d" % li_box[0], shape, dt))
                have_pre = "wi" in pre
                if have_pre:
                    wi, wi_R = pre["wi"], pre["wi_R"]
                else:
                    wi = sb("wiS", [128, 8, 6144], BF16)
                    wi_R = R()
                wo = sb("woS", [128, 16, 1024], BF16)
                wo_R = R()
                gain = sb("gainS", [128, 8], F32)
                gain_R = R()
                vg = sb("vgS", [128, 2048], F32)
                bs = sb("bsS", [128, 2048], F32)
                trl = sb("trlS", [128, 128], F32)
                wsT = sb("wsTS", [128, 16, 128], BF16)
                cst_R = R()
                xt = [sb("xtS%d" % i, [128, 8, NT], F32) for i in range(2)]
                xt_R = [R(), R()]
                hTt = [sb("hTtS%d" % i, [128, 8, NT], BF16) for i in range(2)]
                hTt_R = [R(), R()]
                sq = [sb("sqS%d" % i, [128, NT], F32) for i in range(2)]
                sq_R = [R(), R()]
                lnb = sb("lnbS", [128, NT], F32)
                rstd = sb("rstdS", [128, NT], F32)
                tmpR = [R(), R()]
                vf = sb("vfS", [128, 2048], F32)
                vf_R = R()
                vst = sb("vstS", [128, 4], F32)
                vst_R = R()
                vn = [sb("vnS%d" % i, [128, 2048], BF16) for i in range(2)]
                vn_R = [R(), R()]
                zT = sb("zTS", [128, 16, NT], BF16)
                zT_R = R()
                sg = [sb("sgS%d" % i, [128, NT], F32) for i in range(2)]
                t1 = [sb("t1S%d" % i, [128, NT], F32) for i in range(2)]
                t2 = [sb("t2S%d" % i, [128, NT], F32) for i in range(2)]
                sg_R, t1_R, t2_R = [R(), R()], [R(), R()], [R(), R()]
                wsf = vf[:].rearrange("p (g t) -> p g t", g=16)

                for c in range(8):
                    for q3 in range(3):
                        if have_pre:
                            continue
                        P.dma("gpsimd", wi[:, c, q3 * 2048:(q3 + 1) * 2048], s_win[j, :, c, q3 * 2048:(q3 + 1) * 2048],
                              writes=[wi_R], max_dma_last_dim=2048)
                for q4 in range(4):
                    P.dma("gpsimd", wo[:, q4 * 4:(q4 + 1) * 4, :], s_wout[j, :, q4 * 4:(q4 + 1) * 4, :], writes=[wo_R], max_dma_last_dim=2048)
                P.dma("sync", gain[:], s_norm[j], writes=[gain_R])
                P.dma("sync", vg[:], s_vg[j], writes=[cst_R])
                P.dma("sync", bs[:], s_bs[j], writes=[cst_R])
                P.dma("sync", vf[:], s_ws[j].rearrange("p g t -> p (g t)"), writes=[vf_R])
                P.dma("sync", trl[:], tril, writes=[cst_R])
                tt(P, "vector", wsT[:], wsf, trl[:].unsqueeze(1).broadcast_to([128, 16, 128]), ALU.mult,
                   reads=[cst_R, vf_R], writes=[cst_R])

                NTB = S // NT

                def st_load(tb):
                    k = tb % 2
                    tsl = slice(tb * NT, (tb + 1) * NT)
                    P.dma("sync", xt[k][:], xres_v[:, :, tsl], reads=[xres_R[tb]], writes=[xt_R[k]])

                def st_norm(tb):
                    k = tb % 2
                    norm_fm(xt[k], xt_R[k], NT, gain, gain_R, lambda c, k=k: hTt[k][:, c, :], hTt_R[k],
                            sq, sq_R, lnb, rstd, tmpR, (0, 8))

                def st_v(tb):
                    k = tb % 2
                    for t2i in range(2):
                        for nb in range(4):
                            b = next_bank(0, 8)
                            for c in range(8):
                                mm(P, ps[:, b, :], hTt[k][:, c, t2i * 128:(t2i + 1) * 128],
                                   wi[:, c, 2048 + nb * 512:2048 + (nb + 1) * 512], c == 0, c == 7,
                                   reads=[hTt_R[k], wi_R], writes=[bankR[b]], mark=(c == 7))
                            cp(P, "vector", vf[:, nb * 512:(nb + 1) * 512], ps[:, b, :], reads=[bankR[b]], writes=[vf_R])
                        act(P, vn[t2i][:], vf[:], AF.Square, reads=[vf_R], writes=[vn_R[t2i], vst_R], accum_out=vst[:, 0:1])
                        act(P, vst[:, 1:2], vst[:, 0:1], AF.Ln, reads=[vst_R], writes=[vst_R], scale=1.0 / 2048.0, bias=EPS)
                        act(P, vst[:, 2:3], vst[:, 1:2], AF.Exp, reads=[vst_R], writes=[vst_R], scale=-0.5)
                        stt(P, vn[t2i][:], vf[:], vst[:, 2:3], vg[:], ALU.mult, ALU.mult,
                            reads=[vf_R, vst_R, cst_R], writes=[vn_R[t2i]])

                def st_b(tb):
                    k = tb % 2
                    ug = {}

                    def emit_ug(g):
                        bu = next_bank(0, 8)
                        for c in range(8):
                            mm(P, ps[:, bu, 0:NT], wi[:, c, g * 128:(g + 1) * 128], hTt[k][:, c, :], c == 0, c == 7,
                               reads=[hTt_R[k], wi_R], writes=[bankR[bu]], mark=(c == 7))
                        bg = next_bank(0, 8)
                        for c in range(8):
                            mm(P, ps[:, bg, 0:NT], wi[:, c, 4096 + g * 128:4096 + (g + 1) * 128], hTt[k][:, c, :], c == 0, c == 7,
                               reads=[hTt_R[k], wi_R], writes=[bankR[bg]], mark=(c == 7))
                        kk = g % 2
                        act(P, sg[kk][:], ps[:, bg, 0:NT], AF.Silu, reads=[bankR[bg]], writes=[sg_R[kk]])
                        ug[g] = bu

                    def emit_y(g):
                        kk = g % 2
                        bu = ug.pop(g)
                        by = next_bank(0, 8)
                        for t2i in range(2):
                            mm(P, ps[:, by, t2i * 128:(t2i + 1) * 128], vn[t2i][:, g * 128:(g + 1) * 128], wsT[:, g, :],
                               True, True, reads=[vn_R[t2i], cst_R], writes=[bankR[by]], mark=(t2i == 1))
                        tt(P, "vector", t1[kk][:].rearrange("p (a b) -> p a b", a=2),
                           ps[:, by, 0:NT].rearrange("p (a b) -> p a b", a=2),
                           bs[:, g * 128:(g + 1) * 128].unsqueeze(1).broadcast_to([128, 2, 128]), ALU.add,
                           reads=[bankR[by], cst_R], writes=[t1_R[kk]])
                        tt(P, "vector", t2[kk][:], ps[:, bu, 0:NT], t1[kk][:], ALU.mult,
                           reads=[bankR[bu], t1_R[kk]], writes=[t2_R[kk]])
                        tt(P, "gpsimd", zT[:, g, :], t2[kk][:], sg[kk][:], ALU.mult, reads=[t2_R[kk], sg_R[kk]], writes=[zT_R])

                    LAG = 1
                    for g in range(16 + LAG):
                        if g < 16:
                            emit_ug(g)
                        if g - LAG >= 0:
                            emit_y(g - LAG)

                def st_o(tb):
                    k = tb % 2
                    tsl = slice(tb * NT, (tb + 1) * NT)
                    for c in range(8):
                        b = next_bank(0, 8)
                        for kc in range(16):
                            mm(P, ps[:, b, 0:NT], wo[:, kc, c * 128:(c + 1) * 128], zT[:, kc, :], kc == 0, kc == 15,
                               reads=[wo_R, zT_R], writes=[bankR[b]], mark=(kc == 15))
                        tt(P, "vector", xt[k][:, c, :], ps[:, b, 0:NT], xt[k][:, c, :], ALU.add,
                           reads=[bankR[b]], writes=[xt_R[k]])
                    if final:
                        norm_fm(xt[k], xt_R[k], NT, fnorm_t, constR, lambda c, k=k: xt[k][:, c, :], xt_R[k],
                                sq, sq_R, lnb, rstd, tmpR, (0, 8))
                        P.dma("sync", outT_v[:, :, tsl], xt[k][:], reads=[xt_R[k]], writes=[out_R[tb]])
                    else:
                        P.dma("sync", xres_v[:, :, tsl], xt[k][:], reads=[xt_R[k]], writes=[xres_R[tb]])

                st_load(0)
                st_load(1)
                st_norm(0)
                st_v(0)
                for tb in range(NTB):
                    if tb + 1 < NTB:
                        st_norm(tb + 1)
                    st_b(tb)
                    st_o(tb)
                    if tb + 2 < NTB:
                        st_load(tb + 2)
                    if tb + 1 < NTB:
                        st_v(tb + 1)
                P.drain()
                P.flush()

        for li in range(nlayers):
            j = li // 2
            li_box[0] = li
            if li % 2 == 0:
                attn_layer(j, xT_v if li == 0 else xres_v, li == 0)
            else:
                sgu_layer(j, final=(li == DEPTH - 1))
                if "es" in pre:
                    pre.pop("es").close()
                    pre.clear()
        if nlayers < DEPTH:
            with ExitStack() as pes:
                xt = pes.enter_context(nc.sbuf_tensor("xtD", [128, 8, 512], F32))
                xr = R()
                for tb in range(8):
                    tsl = slice(tb * 512, (tb + 1) * 512)
                    P.dma("sync", xt[:], xres_v[:, :, tsl], reads=[xres_R[2 * tb], xres_R[2 * tb + 1]], writes=[xr])
                    P.dma("sync", outT_v[:, :, tsl], xt[:], reads=[xr], writes=[out_R[tb]])
                P.drain()
                P.flush()
    return nc


def _t5_bucket_np(n):
    n = np.asarray(n, dtype=np.int32)
    max_exact = 16
    nf = np.maximum(n, 1).astype(np.float32)
    large = max_exact + (np.log(nf / np.float32(max_exact)) / np.float32(math.log(128 / max_exact))
                         * np.float32(32 - max_exact)).astype(np.int32)
    large = np.minimum(large, 31)
    return np.where(n < max_exact, n, large)


def make_inputs(inputs):
    f = lambda a: np.ascontiguousarray(np.asarray(a, dtype=np.float32))
    x = f(inputs["x"])
    rel_bias = f(inputs["rel_bias"])
    kl = np.arange(128)[:, None]
    cc = np.arange(256)[None, :]
    n = cc - kl
    bucket = _t5_bucket_np(np.maximum(n, 0))
    G = rel_bias[bucket]
    bmg = f(G.transpose(0, 2, 1).reshape(128, NH * 256))
    rb31 = f(np.broadcast_to(rel_bias[31][None, :], (128, NH)))
    maskc = f(np.where(n >= 0, 0.0, MASKV))
    tril = f((np.arange(128)[:, None] <= np.arange(128)[None, :]).astype(np.float32))

    def pc(v):
        v = f(v)
        return f(v.reshape(v.shape[:-1] + (8, 128)).swapaxes(-1, -2))

    a_win = f(inputs["attn_w_in"]).reshape(2, 8, 128, 4, NH, 128).transpose(0, 4, 2, 1, 3, 5)
    a_win = f(a_win).reshape(2, NH, 128, 8, 512)
    lamv = np.concatenate([f(inputs["attn_lam_q1"]), f(inputs["attn_lam_k1"]),
                           f(inputs["attn_lam_q2"]), f(inputs["attn_lam_k2"])], axis=1)
    a_lam = f(np.broadcast_to(lamv[:, None, :], (2, 128, 256)))
    a_subln = f(inputs["attn_subln"]).reshape(2, 128, 1)
    a_wout = f(f(inputs["attn_w_out"]).reshape(2, 16, 128, 8, 128).transpose(0, 2, 3, 1, 4))
    s_win = f(f(inputs["sgu_w_in"]).reshape(2, 8, 128, 6144).transpose(0, 2, 1, 3))
    s_vg = f(np.broadcast_to(f(inputs["sgu_v_norm"])[:, None, :], (2, 128, 2048)))
    s_ws = f(f(inputs["sgu_w_s"]).transpose(0, 3, 1, 2))
    s_bs = f(np.broadcast_to(f(inputs["sgu_b_s"]).reshape(2, 1, 2048), (2, 128, 2048)))
    s_wout = f(f(inputs["sgu_w_out"]).reshape(2, 16, 128, 1024).transpose(0, 2, 1, 3))
    shared = {
        "bmg": bmg, "rb31": rb31, "maskc": maskc, "tril": tril,
        "a_norm": pc(inputs["attn_norm"]), "a_win": a_win, "a_lam": a_lam, "a_subln": a_subln, "a_wout": a_wout,
        "s_norm": pc(inputs["sgu_norm"]), "s_win": s_win, "s_vg": s_vg, "s_ws": s_ws, "s_bs": s_bs,
        "s_wout": s_wout, "f_norm": pc(inputs["final_norm"]),
    }
    in_maps = []
    for b in range(8):
        m = dict(shared)
        m["xT"] = f(x[b].T)
        in_maps.append(m)
    return in_maps


_NC_CACHE = {}


def kernel(**inputs):
    nl = int(inputs.pop("_nlayers", DEPTH))
    in_maps = make_inputs(inputs)
    if nl not in _NC_CACHE:
        _NC_CACHE[nl] = build_program(nl)
    nc = _NC_CACHE[nl]
    ncores = int(os.environ.get("K_CORES", "8"))
    res = run_bass_kernel_spmd(nc, in_maps[:ncores], core_ids=list(range(ncores)))
    out = np.stack([np.ascontiguousarray(r["outT"].T) for r in res.results], axis=0)
    return out.astype(np.float32)
```

```python
import math
import os
from contextlib import ExitStack

import numpy as np
import concourse.bass as bass
import concourse.mybir as mybir
from concourse.bass_utils import run_bass_kernel_spmd

F32 = mybir.dt.float32
BF16 = mybir.dt.bfloat16
AF = mybir.ActivationFunctionType
ALU = mybir.AluOpType

ENGS = ["tensor", "vector", "scalar", "gpsimd", "sync"]
NDS = 12
S = 4096
D = 1024
NH = 16
EPS = 1e-6
MASKV = -30000.0
DEPTH = 4


class R:
    __slots__ = ("wr", "rd")

    def __init__(self):
        self.wr = None
        self.rd = {}


def _mkwait(sem, v):
    return lambda e: e.wait_ge(sem, v)


def _mkmarked(fn, sem):
    return lambda e: fn(e).then_inc(sem, 1)


def _mkdma(out, in_, sem, kw):
    return lambda e: e.dma_start(out=out, in_=in_, **kw).then_inc(sem, 16)


class Prog:
    def __init__(self, nc, es):
        self.nc = nc
        self.q = {e: [] for e in ENGS}
        self.sem = {e: es.enter_context(nc.semaphore("s_" + e)) for e in ENGS}
        self.cnt = {e: 0 for e in ENGS}
        self.seen = {e: {} for e in ENGS}
        self.dsem = {}
        self.dpool = {}
        for e in ("sync", "gpsimd"):
            keys = []
            for i in range(NDS):
                k = "D%s%d" % (e, i)
                self.dsem[k] = es.enter_context(nc.semaphore(k))
                keys.append(k)
            self.dpool[e] = [keys, [0] * NDS, 0]

    def _waits(self, eng, toks):
        for (key, v) in toks:
            if key == eng and eng == "tensor":
                continue
            if self.seen[eng].get(key, 0) >= v:
                continue
            self.seen[eng][key] = v
            sem = self.sem[key] if key in self.sem else self.dsem[key]
            self.q[eng].append(_mkwait(sem, v))

    @staticmethod
    def _deps(reads, writes):
        toks = []
        for r in reads:
            if r.wr is not None:
                toks.append(r.wr)
        for r in writes:
            if r.wr is not None:
                toks.append(r.wr)
            toks.extend(r.rd.items())
        return toks

    @staticmethod
    def _update(tok, reads, writes):
        k, v = tok
        for r in reads:
            if r.rd.get(k, 0) < v:
                r.rd[k] = v
        for r in writes:
            r.wr = tok
            r.rd = {}

    def op(self, eng, fn, reads=(), writes=(), mark=True):
        self._waits(eng, self._deps(reads, writes))
        if mark:
            self.cnt[eng] += 1
            tok = (eng, self.cnt[eng])
            self.q[eng].append(_mkmarked(fn, self.sem[eng]))
        else:
            assert eng == "tensor"
            tok = (eng, self.cnt[eng] + 1)
            self.q[eng].append(fn)
        self._update(tok, reads, writes)
        return tok

    def dma(self, eng, out, in_, reads=(), writes=(), **kw):
        keys, cnts, nxt = self.dpool[eng]
        i = nxt % NDS
        self.dpool[eng][2] = nxt + 1
        key = keys[i]
        toks = self._deps(reads, writes)
        if cnts[i] > 0:
            toks.append((key, cnts[i]))
        self._waits(eng, toks)
        cnts[i] += 16
        tok = (key, cnts[i])
        self.q[eng].append(_mkdma(out, in_, self.dsem[key], kw))
        self._update(tok, reads, writes)
        return tok

    def drain(self):
        toks = []
        for e in self.dpool:
            keys, cnts, _ = self.dpool[e]
            for k, c in zip(keys, cnts):
                if c > 0:
                    toks.append((k, c))
        self._waits("sync", toks)

    def flush(self):
        nc = self.nc
        with nc.Block() as block:
            for e in ENGS:
                fns = self.q[e]

                def body(eng, fns=fns):
                    for f in fns:
                        f(eng)

                getattr(block, e)(body)
        self.q = {e: [] for e in ENGS}


def mm(P, out, lhsT, rhs, start, stop, reads, writes, mark):
    return P.op("tensor", lambda e: e.matmul(out, lhsT, rhs, start=start, stop=stop),
                reads=reads, writes=writes, mark=mark)


def act(P, out, in_, func, reads, writes, scale=None, bias=None, accum_out=None):
    kw = {}
    if scale is not None:
        kw["scale"] = scale
    if bias is not None:
        kw["bias"] = bias
    if accum_out is not None:
        kw["accum_out"] = accum_out
    return P.op("scalar", lambda e: e.activation(out=out, in_=in_, func=func, **kw), reads=reads, writes=writes)


def tt(P, eng, out, in0, in1, op, reads, writes):
    return P.op(eng, lambda e: e.tensor_tensor(out=out, in0=in0, in1=in1, op=op), reads=reads, writes=writes)


def ts(P, eng, out, in0, s1, s2, op0, op1, reads, writes):
    if op1 is None:
        return P.op(eng, lambda e: e.tensor_scalar(out=out, in0=in0, scalar1=s1, scalar2=None, op0=op0),
                    reads=reads, writes=writes)
    return P.op(eng, lambda e: e.tensor_scalar(out=out, in0=in0, scalar1=s1, scalar2=s2, op0=op0, op1=op1),
                reads=reads, writes=writes)


def stt(P, out, in0, scalar, in1, op0, op1, reads, writes):
    return P.op("vector", lambda e: e.scalar_tensor_tensor(out=out, in0=in0, scalar=scalar, in1=in1, op0=op0, op1=op1),
                reads=reads, writes=writes)


def cp(P, eng, out, in_, reads, writes):
    return P.op(eng, lambda e: e.tensor_copy(out=out, in_=in_), reads=reads, writes=writes)


def build_program(nlayers=DEPTH):
    nc = bass.Bass("TRN2", target_bir_lowering=False)

    def din(name, shape, dt=F32):
        return nc.dram_tensor(name, list(shape), dt, kind="ExternalInput").ap()

    xT = din("xT", [D, S])
    bmg = din("bmg", [128, NH * 256])
    rb31 = din("rb31", [128, NH])
    maskc = din("maskc", [128, 256])
    tril = din("tril", [128, 128])
    a_norm = din("a_norm", [2, 128, 8])
    a_win = din("a_win", [2, NH, 128, 8, 512])
    a_lam = din("a_lam", [2, 128, 256])
    a_subln = din("a_subln", [2, 128, 1])
    a_wout = din("a_wout", [2, 128, 8, 16, 128])
    s_norm = din("s_norm", [2, 128, 8])
    s_win = din("s_win", [2, 128, 8, 6144])
    s_vg = din("s_vg", [2, 128, 2048])
    s_ws = din("s_ws", [2, 128, 16, 128])
    s_bs = din("s_bs", [2, 128, 2048])
    s_wout = din("s_wout", [2, 128, 16, 1024])
    f_norm = din("f_norm", [128, 8])
    outT = nc.dram_tensor("outT", [D, S], F32, kind="ExternalOutput").ap()
    xres = nc.dram_tensor("xres", [D, S], F32, kind="Internal").ap()
    yT = nc.dram_tensor("yT", [2048, S], BF16, kind="Internal").ap()

    xT_v = xT.rearrange("(c p) t -> p c t", p=128)
    xres_v = xres.rearrange("(c p) t -> p c t", p=128)
    outT_v = outT.rearrange("(c p) t -> p c t", p=128)
    yT_v = yT.rearrange("(k p) t -> p k t", p=128)

    xres_R = [R() for _ in range(16)]
    yT_R = [R() for _ in range(8)]
    out_R = [R() for _ in range(16)]

    with ExitStack() as es:
        P = Prog(nc, es)
        ps = es.enter_context(nc.psum_tensor("ps", [128, 8, 512], F32))
        bankR = [R() for _ in range(8)]
        ones_f = es.enter_context(nc.sbuf_tensor("ones_f", [128, 128], F32))
        ones_b = es.enter_context(nc.sbuf_tensor("ones_b", [128, 128], BF16))
        fnorm_t = es.enter_context(nc.sbuf_tensor("fnorm_t", [128, 8], F32))
        constR = R()
        P.op("vector", lambda e: e.memset(ones_f[:], 1.0), writes=[constR])
        P.op("vector", lambda e: e.memset(ones_b[:], 1.0), writes=[constR])
        P.dma("sync", fnorm_t[:], f_norm, writes=[constR])

        rot = {"g": 0}
        li_box = [0]
        pre = {}

        def next_bank(lo=0, hi=8):
            b = lo + rot["g"] % (hi - lo)
            rot["g"] += 1
            return b

        def norm_fm(xt, xt_R, n, gain, gain_R, out_fn, out_R_, sq, sq_R, lnb, rstd, tmpR, banks):
            b = next_bank(*banks)
            for c in range(8):
                k = c % 2
                act(P, sq[k][:, 0:n], xt[:, c, :], AF.Square, reads=[xt_R], writes=[sq_R[k]], scale=1.0 / 32.0)
                mm(P, ps[:, b, 0:n], ones_f[:], sq[k][:, 0:n], c == 0, c == 7,
                   reads=[sq_R[k], constR], writes=[bankR[b]], mark=True)
            act(P, lnb[:, 0:n], ps[:, b, 0:n], AF.Ln, reads=[bankR[b]], writes=[tmpR[0]], scale=1.0, bias=EPS)
            act(P, rstd[:, 0:n], lnb[:, 0:n], AF.Exp, reads=[tmpR[0]], writes=[tmpR[1]], scale=-0.5)
            for c in range(8):
                stt(P, out_fn(c), xt[:, c, :], gain[:, c:c + 1], rstd[:, 0:n], ALU.mult, ALU.mult,
                    reads=[xt_R, gain_R, tmpR[1]], writes=[out_R_])

        stop = os.environ.get("K_STOP", "")

        def attn_layer(j, src_v, src_is_x):
            lam_init = 0.8 - 0.6 * math.exp(-0.3 * (2 * j))
            with ExitStack() as les:
                sb = lambda name, shape, dt: les.enter_context(nc.sbuf_tensor(name + "_L%d" % li_box[0], shape, dt))
                hT = sb("hT", [128, 8, S], BF16)
                hT_R = [R() for _ in range(8)]
                with ExitStack() as pes:
                    sba = lambda name, shape, dt: pes.enter_context(nc.sbuf_tensor(name + "_L%d" % li_box[0], shape, dt))
                    xt = [sba("xtA%d" % i, [128, 8, 512], F32) for i in range(2)]
                    xt_R = [R(), R()]
                    gain = sba("gainA", [128, 8], F32)
                    gain_R = R()
                    sq = [sba("sqA%d" % i, [128, 512], F32) for i in range(2)]
                    sq_R = [R(), R()]
                    lnb = sba("lnbA", [128, 512], F32)
                    rstd = sba("rstdA", [128, 512], F32)
                    tmpR = [R(), R()]
                    P.dma("sync", gain[:], a_norm[j], writes=[gain_R])
                    for tb in range(8):
                        k = tb % 2
                        rd = [] if src_is_x else [xres_R[2 * tb], xres_R[2 * tb + 1]]
                        P.dma("sync", xt[k][:], src_v[:, :, tb * 512:(tb + 1) * 512], reads=rd, writes=[xt_R[k]])
                        norm_fm(xt[k], xt_R[k], 512, gain, gain_R,
                                lambda c, tb=tb: hT[:, c, tb * 512:(tb + 1) * 512], hT_R[tb],
                                sq, sq_R, lnb, rstd, tmpR, (0, 8))
                    P.drain()
                    P.flush()
                if stop == "A":
                    return
                with ExitStack() as pes:
                    sbb = lambda name, shape, dt: pes.enter_context(nc.sbuf_tensor(name + "_L%d" % li_box[0], shape, dt))
                    wh = [sbb("wh%d" % i, [128, 8, 512], BF16) for i in range(2)]
                    wh_R = [R(), R()]
                    QT = sbb("QT", [128, S], BF16)
                    KT = [sbb("KT%d" % i, [128, S], BF16) for i in range(2)]
                    kz_R = R()
                    P.op("gpsimd", lambda e: e.memset(KT[0][64:128, :], 0.0), writes=[kz_R])
                    P.op("gpsimd", lambda e: e.memset(KT[1][0:64, :], 0.0), writes=[kz_R])
                    Vt = sbb("Vt", [128, 32, 128], BF16)
                    GT = sbb("GT", [128, S], F32)
                    QT_R, KT_R, V_R, GT_R = R(), R(), R(), R()
                    BM = sbb("BM", [128, NH, 256], F32)
                    BM_R = R()
                    rb = sbb("rb", [128, NH], F32)
                    mk = sbb("mk", [128, 256], F32)
                    lamt = sbb("lamt", [128, 256], F32)
                    lamp = sbb("lamp", [128, 128], F32)
                    lams = sbb("lams", [128, 8], F32)
                    gsub = sbb("gsub", [128, 2], F32)
                    lam_R = R()
                    Pb = [sbb("Pb%d" % i, [128, 2, 512], BF16) for i in range(4)]
                    Pb_R = [[R(), R()] for _ in range(4)]
                    evq = []
                    r12 = sbb("r12", [128, 2, 512], F32)
                    Lsb = [sbb("Lsb%d" % i, [128, 2, 512], F32) for i in range(2)]
                    O1s = [sbb("O1s%d" % i, [128, 512], F32) for i in range(2)]
                    O2s = [sbb("O2s%d" % i, [128, 512], F32) for i in range(2)]
                    Lsb_R, O1s_R, O2s_R = [R(), R()], [R(), R()], [R(), R()]
                    ea = sbb("ea", [128, 512], F32)
                    eb = sbb("eb", [128, 512], F32)
                    ed = sbb("ed", [128, 512], F32)
                    esq = sbb("esq", [128, 512], F32)
                    eln = sbb("eln", [128, 512], F32)
                    ers = sbb("ers", [128, 512], F32)
                    et = sbb("et", [128, 512], F32)
                    yo = [sbb("yo%d" % i, [128, 512], BF16) for i in range(2)]
                    r12_R, ea_R, eb_R, ed_R, esq_R, eln_R, ers_R, et_R = [R() for _ in range(8)]
                    yo_R = [R(), R()]

                    P.dma("sync", BM[:].rearrange("p h c -> p (h c)"), bmg, writes=[BM_R])
                    P.dma("sync", rb[:], rb31, writes=[lam_R])
                    P.dma("sync", mk[:], maskc, writes=[lam_R])
                    P.dma("sync", lamt[:], a_lam[j], writes=[lam_R])
                    P.dma("sync", gsub[:, 0:1], a_subln[j], writes=[lam_R])
                    for h in range(NH):
                        stt(P, BM[:, h, :], BM[:, h, :], rb[:, h:h + 1], mk[:], ALU.subtract, ALU.add,
                            reads=[lam_R], writes=[BM_R])
                    tt(P, "vector", lamp[:, 0:64], lamt[:, 0:64], lamt[:, 64:128], ALU.mult, reads=[lam_R], writes=[lam_R])
                    tt(P, "vector", lamp[:, 64:128], lamt[:, 128:192], lamt[:, 192:256], ALU.mult, reads=[lam_R], writes=[lam_R])
                    P.op("vector", lambda e: e.reduce_sum(out=lams[:, 0:1], in_=lamp[:, 0:64], axis=mybir.AxisListType.X),
                         reads=[lam_R], writes=[lam_R])
                    P.op("vector", lambda e: e.reduce_sum(out=lams[:, 1:2], in_=lamp[:, 64:128], axis=mybir.AxisListType.X),
                         reads=[lam_R], writes=[lam_R])
                    act(P, lams[:, 2:4], lams[:, 0:2], AF.Exp, reads=[lam_R], writes=[lam_R])
                    tt(P, "vector", lams[:, 4:5], lams[:, 3:4], lams[:, 2:3], ALU.subtract, reads=[lam_R], writes=[lam_R])
                    ts(P, "vector", lams[:, 5:6], lams[:, 4:5], -lam_init, None, ALU.add, None, reads=[lam_R], writes=[lam_R])
                    ts(P, "vector", gsub[:, 1:2], gsub[:, 0:1], 1.0 - lam_init, None, ALU.mult, None, reads=[lam_R], writes=[lam_R])
                    neg_lam = lams[:, 5:6]
                    gs = gsub[:, 1:2]

                    if stop != "B0a":
                        P.dma("gpsimd", wh[0][:], a_win[j, 0], writes=[wh_R[0]], max_dma_last_dim=2048)
                    pbi_box, sp_box, ev_box, tick_box = [0], [0], [0], [0]
                    pending = []

                    def tick():
                        tick_box[0] += 1
                        while pending and pending[0][0] <= tick_box[0]:
                            _, h_, qb_ = pending.pop(0)
                            epilogue2(h_, qb_)

                    for h in range(NH):
                        if stop in ("B0a", "B0b"):
                            break
                        w = wh[h % 2]
                        wR = wh_R[h % 2]
                        if h + 1 < NH:
                            P.dma("gpsimd", wh[(h + 1) % 2][:], a_win[j, h + 1], writes=[wh_R[(h + 1) % 2]], max_dma_last_dim=2048)
                        for tb in range(8):
                            if stop.startswith("B0c") and tb >= int(stop[3:] or 1):
                                break
                            tsl = slice(tb * 512, (tb + 1) * 512)
                            for part in (0, 1, 3):
                                if stop == "B0d" and part != 0:
                                    continue
                                b = next_bank(0, 4)
                                for c in range(8):
                                    mm(P, ps[:, b, :], w[:, c, part * 128:(part + 1) * 128], hT[:, c, tsl],
                                       c == 0, c == 7, reads=[wR, hT_R[tb]], writes=[bankR[b]], mark=(c == 7))
                                if part == 0:
                                    act(P, QT[:, tsl], ps[:, b, :], AF.Copy, reads=[bankR[b]], writes=[QT_R], scale=0.125)
                                elif part == 1:
                                    cp(P, "vector", KT[0][0:64, tsl], ps[0:64, b, :], reads=[bankR[b]], writes=[KT_R])
                                    cp(P, "vector", KT[1][64:128, tsl], ps[64:128, b, :], reads=[bankR[b]], writes=[KT_R])
                                else:
                                    act(P, GT[:, tsl], ps[:, b, :], AF.Silu, reads=[bankR[b]], writes=[GT_R])
                                if h > 0:
                                    tick()
                            b = next_bank(0, 4)
                            for t4 in range(4):
                                tt_ = tb * 4 + t4
                                for c in range(8):
                                    mm(P, ps[:, b, t4 * 128:(t4 + 1) * 128], hT[:, c, tt_ * 128:(tt_ + 1) * 128],
                                       w[:, c, 256:384], c == 0, c == 7, reads=[wR, hT_R[tb]], writes=[bankR[b]],
                                       mark=(c == 7 and t4 == 3))
                            cp(P, "vector", Vt[:, tb * 4:(tb + 1) * 4, :].rearrange("p a b -> p (a b)"), ps[:, b, :],
                               reads=[bankR[b]], writes=[V_R])
                        if stop.startswith("B0"):
                            break
                        nqb = int(stop[1:]) if stop.startswith("Q") else 8
                        its = [(qb, kt) for qb in range(nqb) for kt in range(4 * (qb + 1))]

                        def emit_S(qb, kt):
                            jj = kt - 4 * qb
                            c0 = max(0, 128 * jj)
                            sb0 = 2 * (sp_box[0] % 2)
                            sp_box[0] += 1
                            q0 = qb * 512
                            ksl = slice(kt * 128, (kt + 1) * 128)
                            mm(P, ps[:, sb0, c0:512], KT[0][:, ksl], QT[:, q0 + c0:q0 + 512], True, True,
                               reads=[KT_R, QT_R, kz_R], writes=[bankR[sb0]], mark=False)
                            mm(P, ps[:, sb0 + 1, c0:512], KT[1][:, ksl], QT[:, q0 + c0:q0 + 512], True, True,
                               reads=[KT_R, QT_R, kz_R], writes=[bankR[sb0 + 1]], mark=True)
                            return (sb0, c0, jj)

                        def epilogue1(h, qb):
                            k2 = ev_box[0] % 2
                            ev_box[0] += 1
                            cp(P, "vector", Lsb[k2][:], ps[:, 6:8, :], reads=[bankR[6], bankR[7]], writes=[Lsb_R[k2]])
                            cp(P, "vector", O1s[k2][:], ps[:, 4, :], reads=[bankR[4]], writes=[O1s_R[k2]])
                            cp(P, "vector", O2s[k2][:], ps[:, 5, :], reads=[bankR[5]], writes=[O2s_R[k2]])
                            P.op("vector", lambda e: e.reciprocal(out=r12[:], in_=Lsb[k2][:]),
                                 reads=[Lsb_R[k2]], writes=[r12_R])
                            tt(P, "gpsimd", ea[:], O1s[k2][:], r12[:, 0, :], ALU.mult, reads=[O1s_R[k2], r12_R], writes=[ea_R])
                            tt(P, "gpsimd", eb[:], O2s[k2][:], r12[:, 1, :], ALU.mult, reads=[O2s_R[k2], r12_R], writes=[eb_R])
                            stt(P, ed[:], eb[:], neg_lam, ea[:], ALU.mult, ALU.add, reads=[ea_R, eb_R, lam_R], writes=[ed_R])
                            tt(P, "gpsimd", esq[:], ed[:], ed[:], ALU.mult, reads=[ed_R], writes=[esq_R])

                        def epilogue2(h, qb):
                            q0 = qb * 512
                            sbk = 2 * (sp_box[0] % 2)
                            mm(P, ps[:, sbk, :], ones_f[:], esq[:], True, True, reads=[constR, esq_R],
                               writes=[bankR[sbk], bankR[sbk + 1]], mark=True)
                            act(P, eln[:], ps[:, sbk, :], AF.Ln, reads=[bankR[sbk], bankR[sbk + 1]], writes=[eln_R],
                                scale=1.0 / 128.0, bias=EPS)
                            act(P, ers[:], eln[:], AF.Exp, reads=[eln_R], writes=[ers_R], scale=-0.5)
                            stt(P, et[:], ed[:], gs, ers[:], ALU.mult, ALU.mult, reads=[ed_R, ers_R, lam_R], writes=[et_R])
                            yk = (h * 8 + qb) % 2
                            tt(P, "gpsimd", yo[yk][:], et[:], GT[:, q0:q0 + 512], ALU.mult, reads=[et_R, GT_R],
                               writes=[yo_R[yk]])
                            P.dma("sync", yT_v[:, h, q0:q0 + 512], yo[yk][:], reads=[yo_R[yk]], writes=[yT_R[qb]])

                        st = emit_S(*its[0])
                        for i, (qb, kt) in enumerate(its):
                            nkt = 4 * (qb + 1)
                            sb0, c0, jj = st
                            if i + 1 < len(its):
                                st = emit_S(*its[i + 1])
                            pb = Pb[pbi_box[0] % 4]
                            pR = Pb_R[pbi_box[0] % 4]
                            pbi_box[0] += 1
                            first = kt == 0
                            last = kt == nkt - 1
                            for m in range(2):
                                if jj >= -1:
                                    if jj == -1:
                                        a0, wd, bo = 0, 128, 128
                                    else:
                                        a0, wd, bo = 128 * jj, min(256, 512 - 128 * jj), 0
                                    pv = ps[:, sb0 + m, a0:a0 + wd]
                                    tt(P, "vector", pv, pv, BM[:, h, bo:bo + wd], ALU.add, reads=[BM_R],
                                       writes=[bankR[sb0 + m]])
                                act(P, pb[:, m, c0:512], ps[:, sb0 + m, c0:512], AF.Exp,
                                    reads=[bankR[sb0 + m]], writes=[pR[m]])
                            if evq:
                                epilogue1(*evq.pop(0))
                            for m in range(2):
                                mm(P, ps[:, 4 + m, c0:512], Vt[:, kt, :], pb[:, m, c0:512], first, last,
                                   reads=[V_R, pR[m]], writes=[bankR[4 + m]], mark=False)
                                mm(P, ps[:, 6 + m, c0:512], ones_b[:], pb[:, m, c0:512], first, last,
                                   reads=[constR, pR[m]], writes=[bankR[6 + m]], mark=True)
                            tick()
                            if last:
                                evq.append((h, qb))
                                if i + 1 == len(its):
                                    epilogue1(*evq.pop(0))
                                pending.append((tick_box[0] + (6 if qb == 0 else 12), h, qb))
                        if h == NH - 1 or stop:
                            while pending:
                                _, h_, qb_ = pending.pop(0)
                                epilogue2(h_, qb_)
                        if stop == "B1" or stop.startswith("Q"):
                            break
                    P.drain()
                    P.flush()
                if stop:
                    return
            if li_box[0] + 1 < nlayers:
                wes = ExitStack()
                wi_n = wes.enter_context(nc.sbuf_tensor("wiS_L%d" % (li_box[0] + 1), [128, 8, 6144], BF16))
                pre["wi"], pre["wi_R"], pre["es"] = wi_n, R(), wes
            with ExitStack() as pes:
                sbc = lambda name, shape, dt: pes.enter_context(nc.sbuf_tensor(name + "_L%d" % li_box[0], shape, dt))
                wo = sbc("woC", [128, 8, 16, 128], BF16)
                wo_R = [R() for _ in range(8)]
                yt = [sbc("ytC%d" % i, [128, 16, 512], BF16) for i in range(2)]
                yt_R = [R(), R()]
                xt = [sbc("xtC%d" % i, [128, 8, 512], F32) for i in range(2)]
                xt_R = [R(), R()]
                for c in range(8):
                    P.dma("gpsimd", wo[:, c].rearrange("p k m -> p (k m)"), a_wout[j, :, c].rearrange("p k m -> p (k m)"),
                          writes=[wo_R[c]], max_dma_last_dim=2048)
                if "wi" in pre:
                    for c in range(8):
                        for q3 in range(3):
                            P.dma("gpsimd", pre["wi"][:, c, q3 * 2048:(q3 + 1) * 2048],
                                  s_win[j, :, c, q3 * 2048:(q3 + 1) * 2048], writes=[pre["wi_R"]], max_dma_last_dim=2048)

                def c_load(tb):
                    k = tb % 2
                    tsl = slice(tb * 512, (tb + 1) * 512)
                    rd = [] if src_is_x else [xres_R[2 * tb], xres_R[2 * tb + 1]]
                    P.dma("sync", yt[k][:], yT_v[:, :, tsl], reads=[yT_R[tb]], writes=[yt_R[k]])
                    P.dma("sync", xt[k][:], src_v[:, :, tsl], reads=rd, writes=[xt_R[k]])

                c_load(0)
                for tb in range(8):
                    k = tb % 2
                    tsl = slice(tb * 512, (tb + 1) * 512)
                    if tb + 1 < 8:
                        c_load(tb + 1)
                    for c in range(8):
                        b = next_bank(0, 8)
                        for kc in range(16):
                            mm(P, ps[:, b, :], wo[:, c, kc, :], yt[k][:, kc, :], kc == 0, kc == 15,
                               reads=[wo_R[c], yt_R[k]], writes=[bankR[b]], mark=(kc == 15))
                        tt(P, "vector", xt[k][:, c, :], ps[:, b, :], xt[k][:, c, :], ALU.add,
                           reads=[bankR[b]], writes=[xt_R[k]])
                    P.dma("sync", xres_v[:, :, tsl], xt[k][:], reads=[xt_R[k]],
                          writes=[xres_R[2 * tb], xres_R[2 * tb + 1]])
                P.drain()
                P.flush()

        def sgu_layer(j, final):
            NT = 256
            with ExitStack() as pes:
                sb = lambda name, shape, dt: pes.enter_context(nc.sbuf_tensor(name + "_L%d" % li_box[0], shape, dt))
                have_pre = "wi" in pre
                if have_pre:
                    wi, wi_R = pre["wi"], pre["wi_R"]
                else:
                    wi = sb("wiS", [128, 8, 6144], BF16)
                    wi_R = R()
                wo = sb("woS", [128, 16, 1024], BF16)
                wo_R = R()
                gain = sb("gainS", [128, 8], F32)
                gain_R = R()
                vg = sb("vgS", [128, 2048], F32)
                bs = sb("bsS", [128, 2048], F32)
                trl = sb("trlS", [128, 128], F32)
                wsT = sb("wsTS", [128, 16, 128], BF16)
                cst_R = R()
                xt = [sb("xtS%d" % i, [128, 8, NT], F32) for i in range(2)]
                xt_R = [R(), R()]
                hTt = [sb("hTtS%d" % i, [128, 8, NT], BF16) for i in range(2)]
                hTt_R = [R(), R()]
                sq = [sb("sqS%d" % i, [128, NT], F32) for i in range(2)]
                sq_R = [R(), R()]
                lnb = sb("lnbS", [128, NT], F32)
                rstd = sb("rstdS", [128, NT], F32)
                tmpR = [R(), R()]
                vf = sb("vfS", [128, 2048], F32)
                vf_R = R()
                vst = sb("vstS", [128, 4], F32)
                vst_R = R()
                vn = [sb("vnS%d" % i, [128, 2048], BF16) for i in range(2)]
                vn_R = [R(), R()]
                zT = sb("zTS", [128, 16, NT], BF16)
                zT_R = R()
                sg = [sb("sgS%d" % i, [128, NT], F32) for i in range(2)]
                t1 = [sb("t1S%d" % i, [128, NT], F32) for i in range(2)]
                t2 = [sb("t2S%d" % i, [128, NT], F32) for i in range(2)]
                sg_R, t1_R, t2_R = [R(), R()], [R(), R()], [R(), R()]
                wsf = vf[:].rearrange("p (g t) -> p g t", g=16)

                for c in range(8):
                    for q3 in range(3):
                        if have_pre:
                            continue
                        P.dma("gpsimd", wi[:, c, q3 * 2048:(q3 + 1) * 2048], s_win[j, :, c, q3 * 2048:(q3 + 1) * 2048],
                              writes=[wi_R], max_dma_last_dim=2048)
                for q4 in range(4):
                    P.dma("gpsimd", wo[:, q4 * 4:(q4 + 1) * 4, :], s_wout[j, :, q4 * 4:(q4 + 1) * 4, :], writes=[wo_R], max_dma_last_dim=2048)
                P.dma("sync", gain[:], s_norm[j], writes=[gain_R])
                P.dma("sync", vg[:], s_vg[j], writes=[cst_R])
                P.dma("sync", bs[:], s_bs[j], writes=[cst_R])
                P.dma("sync", vf[:], s_ws[j].rearrange("p g t -> p (g t)"), writes=[vf_R])
                P.dma("sync", trl[:], tril, writes=[cst_R])
                tt(P, "vector", wsT[:], wsf, trl[:].unsqueeze(1).broadcast_to([128, 16, 128]), ALU.mult,
                   reads=[cst_R, vf_R], writes=[cst_R])

                NTB = S // NT

                def st_load(tb):
                    k = tb % 2
                    tsl = slice(tb * NT, (tb + 1) * NT)
                    P.dma("sync", xt[k][:], xres_v[:, :, tsl], reads=[xres_R[tb]], writes=[xt_R[k]])

                def st_norm(tb):
                    k = tb % 2
                    norm_fm(xt[k], xt_R[k], NT, gain, gain_R, lambda c, k=k: hTt[k][:, c, :], hTt_R[k],
                            sq, sq_R, lnb, rstd, tmpR, (0, 8))

                def st_v(tb):
                    k = tb % 2
                    for t2i in range(2):
                        for nb in range(4):
                            b = next_bank(0, 8)
                            for c in range(8):
                                mm(P, ps[:, b, :], hTt[k][:, c, t2i * 128:(t2i + 1) * 128],
                                   wi[:, c, 2048 + nb * 512:2048 + (nb + 1) * 512], c == 0, c == 7,
                                   reads=[hTt_R[k], wi_R], writes=[bankR[b]], mark=(c == 7))
                            cp(P, "vector", vf[:, nb * 512:(nb + 1) * 512], ps[:, b, :], reads=[bankR[b]], writes=[vf_R])
                        act(P, vn[t2i][:], vf[:], AF.Square, reads=[vf_R], writes=[vn_R[t2i], vst_R], accum_out=vst[:, 0:1])
                        act(P, vst[:, 1:2], vst[:, 0:1], AF.Ln, reads=[vst_R], writes=[vst_R], scale=1.0 / 2048.0, bias=EPS)
                        act(P, vst[:, 2:3], vst[:, 1:2], AF.Exp, reads=[vst_R], writes=[vst_R], scale=-0.5)
                        stt(P, vn[t2i][:], vf[:], vst[:, 2:3], vg[:], ALU.mult, ALU.mult,
                            reads=[vf_R, vst_R, cst_R], writes=[vn_R[t2i]])

                def st_b(tb):
                    k = tb % 2
                    ug = {}

                    def emit_ug(g):
                        bu = next_bank(0, 8)
                        for c in range(8):
                            mm(P, ps[:, bu, 0:NT], wi[:, c, g * 128:(g + 1) * 128], hTt[k][:, c, :], c == 0, c == 7,
                               reads=[hTt_R[k], wi_R], writes=[bankR[bu]], mark=(c == 7))
                        bg = next_bank(0, 8)
                        for c in range(8):
                            mm(P, ps[:, bg, 0:NT], wi[:, c, 4096 + g * 128:4096 + (g + 1) * 128], hTt[k][:, c, :], c == 0, c == 7,
                               reads=[hTt_R[k], wi_R], writes=[bankR[bg]], mark=(c == 7))
                        kk = g % 2
                        act(P, sg[kk][:], ps[:, bg, 0:NT], AF.Silu, reads=[bankR[bg]], writes=[sg_R[kk]])
                        ug[g] = bu

                    def emit_y(g):
                        kk = g % 2
                        bu = ug.pop(g)
                        by = next_bank(0, 8)
                        for t2i in range(2):
                            mm(P, ps[:, by, t2i * 128:(t2i + 1) * 128], vn[t2i][:, g * 128:(g + 1) * 128], wsT[:, g, :],
                               True, True, reads=[vn_R[t2i], cst_R], writes=[bankR[by]], mark=(t2i == 1))
                        tt(P, "vector", t1[kk][:].rearrange("p (a b) -> p a b", a=2),
                           ps[:, by, 0:NT].rearrange("p (a b) -> p a b", a=2),
                           bs[:, g * 128:(g + 1) * 128].unsqueeze(1).broadcast_to([128, 2, 128]), ALU.add,
                           reads=[bankR[by], cst_R], writes=[t1_R[kk]])
                        tt(P, "vector", t2[kk][:], ps[:, bu, 0:NT], t1[kk][:], ALU.mult,
                           reads=[bankR[bu], t1_R[kk]], writes=[t2_R[kk]])
                        tt(P, "gpsimd", zT[:, g, :], t2[kk][:], sg[kk][:], ALU.mult, reads=[t2_R[kk], sg_R[kk]], writes=[zT_R])

                    LAG = 1
                    for g in range(16 + LAG):
                        if g < 16:
                            emit_ug(g)
                        if g - LAG >= 0:
                            emit_y(g - LAG)

                def st_o(tb):
                    k = tb % 2
                    tsl = slice(tb * NT, (tb + 1) * NT)
                    for c in range(8):
                        b = next_bank(0, 8)
                        for kc in range(16):
                            mm(P, ps[:, b, 0:NT], wo[:, kc, c * 128:(c + 1) * 128], zT[:, kc, :], kc == 0, kc == 15,
                               reads=[wo_R, zT_R], writes=[bankR[b]], mark=(kc == 15))
                        tt(P, "vector", xt[k][:, c, :], ps[:, b, 0:NT], xt[k][:, c, :], ALU.add,
                           reads=[bankR[b]], writes=[xt_R[k]])
                    if final:
                        norm_fm(xt[k], xt_R[k], NT, fnorm_t, constR, lambda c, k=k: xt[k][:, c, :], xt_R[k],
                                sq, sq_R, lnb, rstd, tmpR, (0, 8))
                        P.dma("sync", outT_v[:, :, tsl], xt[k][:], reads=[xt_R[k]], writes=[out_R[tb]])
                    else:
                        P.dma("sync", xres_v[:, :, tsl], xt[k][:], reads=[xt_R[k]], writes=[xres_R[tb]])

                st_load(0)
                st_load(1)
                st_norm(0)
                st_v(0)
                for tb in range(NTB):
                    if tb + 1 < NTB:
                        st_norm(tb + 1)
                    st_b(tb)
                    st_o(tb)
                    if tb + 2 < NTB:
                        st_load(tb + 2)
                    if tb + 1 < NTB:
                        st_v(tb + 1)
                P.drain()
                P.flush()

        for li in range(nlayers):
            j = li // 2
            li_box[0] = li
            if li % 2 == 0:
                attn_layer(j, xT_v if li == 0 else xres_v, li == 0)
            else:
                sgu_layer(j, final=(li == DEPTH - 1))
                if "es" in pre:
                    pre.pop("es").close()
                    pre.clear()
        if nlayers < DEPTH:
            with ExitStack() as pes:
                xt = pes.enter_context(nc.sbuf_tensor("xtD", [128, 8, 512], F32))
                xr = R()
                for tb in range(8):
                    tsl = slice(tb * 512, (tb + 1) * 512)
                    P.dma("sync", xt[:], xres_v[:, :, tsl], reads=[xres_R[2 * tb], xres_R[2 * tb + 1]], writes=[xr])
                    P.dma("sync", outT_v[:, :, tsl], xt[:], reads=[xr], writes=[out_R[tb]])
                P.drain()
                P.flush()
    return nc


def _t5_bucket_np(n):
    n = np.asarray(n, dtype=np.int32)
    max_exact = 16
    nf = np.maximum(n, 1).astype(np.float32)
    large = max_exact + (np.log(nf / np.float32(max_exact)) / np.float32(math.log(128 / max_exact))
                         * np.float32(32 - max_exact)).astype(np.int32)
    large = np.minimum(large, 31)
    return np.where(n < max_exact, n, large)


def make_inputs(inputs):
    f = lambda a: np.ascontiguousarray(np.asarray(a, dtype=np.float32))
    x = f(inputs["x"])
    rel_bias = f(inputs["rel_bias"])
    kl = np.arange(128)[:, None]
    cc = np.arange(256)[None, :]
    n = cc - kl
    bucket = _t5_bucket_np(np.maximum(n, 0))
    G = rel_bias[bucket]
    bmg = f(G.transpose(0, 2, 1).reshape(128, NH * 256))
    rb31 = f(np.broadcast_to(rel_bias[31][None, :], (128, NH)))
    maskc = f(np.where(n >= 0, 0.0, MASKV))
    tril = f((np.arange(128)[:, None] <= np.arange(128)[None, :]).astype(np.float32))

    def pc(v):
        v = f(v)
        return f(v.reshape(v.shape[:-1] + (8, 128)).swapaxes(-1, -2))

    a_win = f(inputs["attn_w_in"]).reshape(2, 8, 128, 4, NH, 128).transpose(0, 4, 2, 1, 3, 5)
    a_win = f(a_win).reshape(2, NH, 128, 8, 512)
    lamv = np.concatenate([f(inputs["attn_lam_q1"]), f(inputs["attn_lam_k1"]),
                           f(inputs["attn_lam_q2"]), f(inputs["attn_lam_k2"])], axis=1)
    a_lam = f(np.broadcast_to(lamv[:, None, :], (2, 128, 256)))
    a_subln = f(inputs["attn_subln"]).reshape(2, 128, 1)
    a_wout = f(f(inputs["attn_w_out"]).reshape(2, 16, 128, 8, 128).transpose(0, 2, 3, 1, 4))
    s_win = f(f(inputs["sgu_w_in"]).reshape(2, 8, 128, 6144).transpose(0, 2, 1, 3))
    s_vg = f(np.broadcast_to(f(inputs["sgu_v_norm"])[:, None, :], (2, 128, 2048)))
    s_ws = f(f(inputs["sgu_w_s"]).transpose(0, 3, 1, 2))
    s_bs = f(np.broadcast_to(f(inputs["sgu_b_s"]).reshape(2, 1, 2048), (2, 128, 2048)))
    s_wout = f(f(inputs["sgu_w_out"]).reshape(2, 16, 128, 1024).transpose(0, 2, 1, 3))
    shared = {
        "bmg": bmg, "rb31": rb31, "maskc": maskc, "tril": tril,
        "a_norm": pc(inputs["attn_norm"]), "a_win": a_win, "a_lam": a_lam, "a_subln": a_subln, "a_wout": a_wout,
        "s_norm": pc(inputs["sgu_norm"]), "s_win": s_win, "s_vg": s_vg, "s_ws": s_ws, "s_bs": s_bs,
        "s_wout": s_wout, "f_norm": pc(inputs["final_norm"]),
    }
    in_maps = []
    for b in range(8):
        m = dict(shared)
        m["xT"] = f(x[b].T)
        in_maps.append(m)
    return in_maps


_NC_CACHE = {}


def kernel(**inputs):
    nl = int(inputs.pop("_nlayers", DEPTH))
    in_maps = make_inputs(inputs)
    if nl not in _NC_CACHE:
        _NC_CACHE[nl] = build_program(nl)
    nc = _NC_CACHE[nl]
    ncores = int(os.environ.get("K_CORES", "8"))
    res = run_bass_kernel_spmd(nc, in_maps[:ncores], core_ids=list(range(ncores)))
    out = np.stack([np.ascontiguousarray(r["outT"].T) for r in res.results], axis=0)
    return out.astype(np.float32)
```

```python
import math
import os
from contextlib import ExitStack

import numpy as np
import concourse.bass as bass
import concourse.mybir as mybir
from concourse.bass_utils import run_bass_kernel_spmd

F32 = mybir.dt.float32
BF16 = mybir.dt.bfloat16
AF = mybir.ActivationFunctionType
ALU = mybir.AluOpType

ENGS = ["tensor", "vector", "scalar", "gpsimd", "sync"]
NDS = 12
S = 4096
D = 1024
NH = 16
EPS = 1e-6
MASKV = -30000.0
DEPTH = 4


class R:
    __slots__ = ("wr", "rd")

    def __init__(self):
        self.wr = None
        self.rd = {}


def _mkwait(sem, v):
    return lambda e: e.wait_ge(sem, v)


def _mkmarked(fn, sem):
    return lambda e: fn(e).then_inc(sem, 1)


def _mkdma(out, in_, sem, kw):
    return lambda e: e.dma_start(out=out, in_=in_, **kw).then_inc(sem, 16)


class Prog:
    def __init__(self, nc, es):
        self.nc = nc
        self.q = {e: [] for e in ENGS}
        self.sem = {e: es.enter_context(nc.semaphore("s_" + e)) for e in ENGS}
        self.cnt = {e: 0 for e in ENGS}
        self.seen = {e: {} for e in ENGS}
        self.dsem = {}
        self.dpool = {}
        for e in ("sync", "gpsimd"):
            keys = []
            for i in range(NDS):
                k = "D%s%d" % (e, i)
                self.dsem[k] = es.enter_context(nc.semaphore(k))
                keys.append(k)
            self.dpool[e] = [keys, [0] * NDS, 0]

    def _waits(self, eng, toks):
        for (key, v) in toks:
            if key == eng and eng == "tensor":
                continue
            if self.seen[eng].get(key, 0) >= v:
                continue
            self.seen[eng][key] = v
            sem = self.sem[key] if key in self.sem else self.dsem[key]
            self.q[eng].append(_mkwait(sem, v))

    @staticmethod
    def _deps(reads, writes):
        toks = []
        for r in reads:
            if r.wr is not None:
                toks.append(r.wr)
        for r in writes:
            if r.wr is not None:
                toks.append(r.wr)
            toks.extend(r.rd.items())
        return toks

    @staticmethod
    def _update(tok, reads, writes):
        k, v = tok
        for r in reads:
            if r.rd.get(k, 0) < v:
                r.rd[k] = v
        for r in writes:
            r.wr = tok
            r.rd = {}

    def op(self, eng, fn, reads=(), writes=(), mark=True):
        self._waits(eng, self._deps(reads, writes))
        if mark:
            self.cnt[eng] += 1
            tok = (eng, self.cnt[eng])
            self.q[eng].append(_mkmarked(fn, self.sem[eng]))
        else:
            assert eng == "tensor"
            tok = (eng, self.cnt[eng] + 1)
            self.q[eng].append(fn)
        self._update(tok, reads, writes)
        return tok

    def dma(self, eng, out, in_, reads=(), writes=(), **kw):
        keys, cnts, nxt = self.dpool[eng]
        i = nxt % NDS
        self.dpool[eng][2] = nxt + 1
        key = keys[i]
        toks = self._deps(reads, writes)
        if cnts[i] > 0:
            toks.append((key, cnts[i]))
        self._waits(eng, toks)
        cnts[i] += 16
        tok = (key, cnts[i])
        self.q[eng].append(_mkdma(out, in_, self.dsem[key], kw))
        self._update(tok, reads, writes)
        return tok

    def drain(self):
        toks = []
        for e in self.dpool:
            keys, cnts, _ = self.dpool[e]
            for k, c in zip(keys, cnts):
                if c > 0:
                    toks.append((k, c))
        self._waits("sync", toks)

    def flush(self):
        nc = self.nc
        with nc.Block() as block:
            for e in ENGS:
                fns = self.q[e]

                def body(eng, fns=fns):
                    for f in fns:
                        f(eng)

                getattr(block, e)(body)
        self.q = {e: [] for e in ENGS}


def mm(P, out, lhsT, rhs, start, stop, reads, writes, mark):
    return P.op("tensor", lambda e: e.matmul(out, lhsT, rhs, start=start, stop=stop),
                reads=reads, writes=writes, mark=mark)


def act(P, out, in_, func, reads, writes, scale=None, bias=None, accum_out=None):
    kw = {}
    if scale is not None:
        kw["scale"] = scale
    if bias is not None:
        kw["bias"] = bias
    if accum_out is not None:
        kw["accum_out"] = accum_out
    return P.op("scalar", lambda e: e.activation(out=out, in_=in_, func=func, **kw), reads=reads, writes=writes)


def tt(P, eng, out, in0, in1, op, reads, writes):
    return P.op(eng, lambda e: e.tensor_tensor(out=out, in0=in0, in1=in1, op=op), reads=reads, writes=writes)


def ts(P, eng, out, in0, s1, s2, op0, op1, reads, writes):
    if op1 is None:
        return P.op(eng, lambda e: e.tensor_scalar(out=out, in0=in0, scalar1=s1, scalar2=None, op0=op0),
                    reads=reads, writes=writes)
    return P.op(eng, lambda e: e.tensor_scalar(out=out, in0=in0, scalar1=s1, scalar2=s2, op0=op0, op1=op1),
                reads=reads, writes=writes)


def stt(P, out, in0, scalar, in1, op0, op1, reads, writes):
    return P.op("vector", lambda e: e.scalar_tensor_tensor(out=out, in0=in0, scalar=scalar, in1=in1, op0=op0, op1=op1),
                reads=reads, writes=writes)


def cp(P, eng, out, in_, reads, writes):
    return P.op(eng, lambda e: e.tensor_copy(out=out, in_=in_), reads=reads, writes=writes)


def build_program(nlayers=DEPTH):
    nc = bass.Bass("TRN2", target_bir_lowering=False)

    def din(name, shape, dt=F32):
        return nc.dram_tensor(name, list(shape), dt, kind="ExternalInput").ap()

    xT = din("xT", [D, S])
    bmg = din("bmg", [128, NH * 256])
    rb31 = din("rb31", [128, NH])
    maskc = din("maskc", [128, 256])
    tril = din("tril", [128, 128])
    a_norm = din("a_norm", [2, 128, 8])
    a_win = din("a_win", [2, NH, 128, 8, 512])
    a_lam = din("a_lam", [2, 128, 256])
    a_subln = din("a_subln", [2, 128, 1])
    a_wout = din("a_wout", [2, 128, 8, 16, 128])
    s_norm = din("s_norm", [2, 128, 8])
    s_win = din("s_win", [2, 128, 8, 6144])
    s_vg = din("s_vg", [2, 128, 2048])
    s_ws = din("s_ws", [2, 128, 16, 128])
    s_bs = din("s_bs", [2, 128, 2048])
    s_wout = din("s_wout", [2, 128, 16, 1024])
    f_norm = din("f_norm", [128, 8])
    outT = nc.dram_tensor("outT", [D, S], F32, kind="ExternalOutput").ap()
    xres = nc.dram_tensor("xres", [D, S], F32, kind="Internal").ap()
    yT = nc.dram_tensor("yT", [2048, S], BF16, kind="Internal").ap()

    xT_v = xT.rearrange("(c p) t -> p c t", p=128)
    xres_v = xres.rearrange("(c p) t -> p c t", p=128)
    outT_v = outT.rearrange("(c p) t -> p c t", p=128)
    yT_v = yT.rearrange("(k p) t -> p k t", p=128)

    xres_R = [R() for _ in range(16)]
    yT_R = [R() for _ in range(8)]
    out_R = [R() for _ in range(16)]

    with ExitStack() as es:
        P = Prog(nc, es)
        ps = es.enter_context(nc.psum_tensor("ps", [128, 8, 512], F32))
        bankR = [R() for _ in range(8)]
        ones_f = es.enter_context(nc.sbuf_tensor("ones_f", [128, 128], F32))
        ones_b = es.enter_context(nc.sbuf_tensor("ones_b", [128, 128], BF16))
        fnorm_t = es.enter_context(nc.sbuf_tensor("fnorm_t", [128, 8], F32))
        constR = R()
        P.op("vector", lambda e: e.memset(ones_f[:], 1.0), writes=[constR])
        P.op("vector", lambda e: e.memset(ones_b[:], 1.0), writes=[constR])
        P.dma("sync", fnorm_t[:], f_norm, writes=[constR])

        rot = {"g": 0}
        li_box = [0]
        pre = {}

        def next_bank(lo=0, hi=8):
            b = lo + rot["g"] % (hi - lo)
            rot["g"] += 1
            return b

        def norm_fm(xt, xt_R, n, gain, gain_R, out_fn, out_R_, sq, sq_R, lnb, rstd, tmpR, banks):
            b = next_bank(*banks)
            for c in range(8):
                k = c % 2
                act(P, sq[k][:, 0:n], xt[:, c, :], AF.Square, reads=[xt_R], writes=[sq_R[k]], scale=1.0 / 32.0)
                mm(P, ps[:, b, 0:n], ones_f[:], sq[k][:, 0:n], c == 0, c == 7,
                   reads=[sq_R[k], constR], writes=[bankR[b]], mark=True)
            act(P, lnb[:, 0:n], ps[:, b, 0:n], AF.Ln, reads=[bankR[b]], writes=[tmpR[0]], scale=1.0, bias=EPS)
            act(P, rstd[:, 0:n], lnb[:, 0:n], AF.Exp, reads=[tmpR[0]], writes=[tmpR[1]], scale=-0.5)
            for c in range(8):
                stt(P, out_fn(c), xt[:, c, :], gain[:, c:c + 1], rstd[:, 0:n], ALU.mult, ALU.mult,
                    reads=[xt_R, gain_R, tmpR[1]], writes=[out_R_])

        stop = os.environ.get("K_STOP", "")

        def attn_layer(j, src_v, src_is_x):
            lam_init = 0.8 - 0.6 * math.exp(-0.3 * (2 * j))
            with ExitStack() as les:
                sb = lambda name, shape, dt: les.enter_context(nc.sbuf_tensor(name + "_L%d" % li_box[0], shape, dt))
                hT = sb("hT", [128, 8, S], BF16)
                hT_R = [R() for _ in range(8)]
                with ExitStack() as pes:
                    sba = lambda name, shape, dt: pes.enter_context(nc.sbuf_tensor(name + "_L%d" % li_box[0], shape, dt))
                    xt = [sba("xtA%d" % i, [128, 8, 512], F32) for i in range(2)]
                    xt_R = [R(), R()]
                    gain = sba("gainA", [128, 8], F32)
                    gain_R = R()
                    sq = [sba("sqA%d" % i, [128, 512], F32) for i in range(2)]
                    sq_R = [R(), R()]
                    lnb = sba("lnbA", [128, 512], F32)
                    rstd = sba("rstdA", [128, 512], F32)
                    tmpR = [R(), R()]
                    P.dma("sync", gain[:], a_norm[j], writes=[gain_R])
                    for tb in range(8):
                        k = tb % 2
                        rd = [] if src_is_x else [xres_R[2 * tb], xres_R[2 * tb + 1]]
                        P.dma("sync", xt[k][:], src_v[:, :, tb * 512:(tb + 1) * 512], reads=rd, writes=[xt_R[k]])
                        norm_fm(xt[k], xt_R[k], 512, gain, gain_R,
                                lambda c, tb=tb: hT[:, c, tb * 512:(tb + 1) * 512], hT_R[tb],
                                sq, sq_R, lnb, rstd, tmpR, (0, 8))
                    P.drain()
                    P.flush()
                if stop == "A":
                    return
                with ExitStack() as pes:
                    sbb = lambda name, shape, dt: pes.enter_context(nc.sbuf_tensor(name + "_L%d" % li_box[0], shape, dt))
                    wh = [sbb("wh%d" % i, [128, 8, 512], BF16) for i in range(2)]
                    wh_R = [R(), R()]
                    QT = sbb("QT", [128, S], BF16)
                    KT = [sbb("KT%d" % i, [128, S], BF16) for i in range(2)]
                    kz_R = R()
                    P.op("gpsimd", lambda e: e.memset(KT[0][64:128, :], 0.0), writes=[kz_R])
                    P.op("gpsimd", lambda e: e.memset(KT[1][0:64, :], 0.0), writes=[kz_R])
                    Vt = sbb("Vt", [128, 32, 128], BF16)
                    GT = sbb("GT", [128, S], F32)
                    QT_R, KT_R, V_R, GT_R = R(), R(), R(), R()
                    BM = sbb("BM", [128, NH, 256], F32)
                    BM_R = R()
                    rb = sbb("rb", [128, NH], F32)
                    mk = sbb("mk", [128, 256], F32)
                    lamt = sbb("lamt", [128, 256], F32)
                    lamp = sbb("lamp", [128, 128], F32)
                    lams = sbb("lams", [128, 8], F32)
                    gsub = sbb("gsub", [128, 2], F32)
                    lam_R = R()
                    Pb = [sbb("Pb%d" % i, [128, 2, 512], BF16) for i in range(4)]
                    Pb_R = [[R(), R()] for _ in range(4)]
                    evq = []
                    r12 = sbb("r12", [128, 2, 512], F32)
                    Lsb = [sbb("Lsb%d" % i, [128, 2, 512], F32) for i in range(2)]
                    O1s = [sbb("O1s%d" % i, [128, 512], F32) for i in range(2)]
                    O2s = [sbb("O2s%d" % i, [128, 512], F32) for i in range(2)]
                    Lsb_R, O1s_R, O2s_R = [R(), R()], [R(), R()], [R(), R()]
                    ea = sbb("ea", [128, 512], F32)
                    eb = sbb("eb", [128, 512], F32)
                    ed = sbb("ed", [128, 512], F32)
                    esq = sbb("esq", [128, 512], F32)
                    eln = sbb("eln", [128, 512], F32)
                    ers = sbb("ers", [128, 512], F32)
                    et = sbb("et", [128, 512], F32)
                    yo = [sbb("yo%d" % i, [128, 512], BF16) for i in range(2)]
                    r12_R, ea_R, eb_R, ed_R, esq_R, eln_R, ers_R, et_R = [R() for _ in range(8)]
                    yo_R = [R(), R()]

                    P.dma("sync", BM[:].rearrange("p h c -> p (h c)"), bmg, writes=[BM_R])
                    P.dma("sync", rb[:], rb31, writes=[lam_R])
                    P.dma("sync", mk[:], maskc, writes=[lam_R])
                    P.dma("sync", lamt[:], a_lam[j], writes=[lam_R])
                    P.dma("sync", gsub[:, 0:1], a_subln[j], writes=[lam_R])
                    for h in range(NH):
                        stt(P, BM[:, h, :], BM[:, h, :], rb[:, h:h + 1], mk[:], ALU.subtract, ALU.add,
                            reads=[lam_R], writes=[BM_R])
                    tt(P, "vector", lamp[:, 0:64], lamt[:, 0:64], lamt[:, 64:128], ALU.mult, reads=[lam_R], writes=[lam_R])
                    tt(P, "vector", lamp[:, 64:128], lamt[:, 128:192], lamt[:, 192:256], ALU.mult, reads=[lam_R], writes=[lam_R])
                    P.op("vector", lambda e: e.reduce_sum(out=lams[:, 0:1], in_=lamp[:, 0:64], axis=mybir.AxisListType.X),
                         reads=[lam_R], writes=[lam_R])
                    P.op("vector", lambda e: e.reduce_sum(out=lams[:, 1:2], in_=lamp[:, 64:128], axis=mybir.AxisListType.X),
                         reads=[lam_R], writes=[lam_R])
                    act(P, lams[:, 2:4], lams[:, 0:2], AF.Exp, reads=[lam_R], writes=[lam_R])
                    tt(P, "vector", lams[:, 4:5], lams[:, 3:4], lams[:, 2:3], ALU.subtract, reads=[lam_R], writes=[lam_R])
                    ts(P, "vector", lams[:, 5:6], lams[:, 4:5], -lam_init, None, ALU.add, None, reads=[lam_R], writes=[lam_R])
                    ts(P, "vector", gsub[:, 1:2], gsub[:, 0:1], 1.0 - lam_init, None, ALU.mult, None, reads=[lam_R], writes=[lam_R])
                    neg_lam = lams[:, 5:6]
                    gs = gsub[:, 1:2]

                    if stop != "B0a":
                        P.dma("gpsimd", wh[0][:], a_win[j, 0], writes=[wh_R[0]], max_dma_last_dim=2048)
                    pbi_box, sp_box, ev_box, tick_box = [0], [0], [0], [0]
                    pending = []

                    def tick():
                        tick_box[0] += 1
                        while pending and pending[0][0] <= tick_box[0]:
                            _, h_, qb_ = pending.pop(0)
                            epilogue2(h_, qb_)

                    for h in range(NH):
                        if stop in ("B0a", "B0b"):
                            break
                        w = wh[h % 2]
                        wR = wh_R[h % 2]
                        if h + 1 < NH:
                            P.dma("gpsimd", wh[(h + 1) % 2][:], a_win[j, h + 1], writes=[wh_R[(h + 1) % 2]], max_dma_last_dim=2048)
                        for tb in range(8):
                            if stop.startswith("B0c") and tb >= int(stop[3:] or 1):
                                break
                            tsl = slice(tb * 512, (tb + 1) * 512)
                            for part in (0, 1, 3):
                                if stop == "B0d" and part != 0:
                                    continue
                                b = next_bank(0, 4)
                                for c in range(8):
                                    mm(P, ps[:, b, :], w[:, c, part * 128:(part + 1) * 128], hT[:, c, tsl],
                                       c == 0, c == 7, reads=[wR, hT_R[tb]], writes=[bankR[b]], mark=(c == 7))
                                if part == 0:
                                    act(P, QT[:, tsl], ps[:, b, :], AF.Copy, reads=[bankR[b]], writes=[QT_R], scale=0.125)
                                elif part == 1:
                                    cp(P, "vector", KT[0][0:64, tsl], ps[0:64, b, :], reads=[bankR[b]], writes=[KT_R])
                                    cp(P, "vector", KT[1][64:128, tsl], ps[64:128, b, :], reads=[bankR[b]], writes=[KT_R])
                                else:
                                    act(P, GT[:, tsl], ps[:, b, :], AF.Silu, reads=[bankR[b]], writes=[GT_R])
                                if h > 0:
                                    tick()
                            b = next_bank(0, 4)
                            for t4 in range(4):
                                tt_ = tb * 4 + t4
                                for c in range(8):
                                    mm(P, ps[:, b, t4 * 128:(t4 + 1) * 128], hT[:, c, tt_ * 128:(tt_ + 1) * 128],
                                       w[:, c, 256:384], c == 0, c == 7, reads=[wR, hT_R[tb]], writes=[bankR[b]],
                                       mark=(c == 7 and t4 == 3))
                            cp(P, "vector", Vt[:, tb * 4:(tb + 1) * 4, :].rearrange("p a b -> p (a b)"), ps[:, b, :],
                               reads=[bankR[b]], writes=[V_R])
                        if stop.startswith("B0"):
                            break
                        nqb = int(stop[1:]) if stop.startswith("Q") else 8
                        its = [(qb, kt) for qb in range(nqb) for kt in range(4 * (qb + 1))]

                        def emit_S(qb, kt):
                            jj = kt - 4 * qb
                            c0 = max(0, 128 * jj)
                            sb0 = 2 * (sp_box[0] % 2)
                            sp_box[0] += 1
                            q0 = qb * 512
                            ksl = slice(kt * 128, (kt + 1) * 128)
                            mm(P, ps[:, sb0, c0:512], KT[0][:, ksl], QT[:, q0 + c0:q0 + 512], True, True,
                               reads=[KT_R, QT_R, kz_R], writes=[bankR[sb0]], mark=False)
                            mm(P, ps[:, sb0 + 1, c0:512], KT[1][:, ksl], QT[:, q0 + c0:q0 + 512], True, True,
                               reads=[KT_R, QT_R, kz_R], writes=[bankR[sb0 + 1]], mark=True)
                            return (sb0, c0, jj)

                        def epilogue1(h, qb):
                            k2 = ev_box[0] % 2
                            ev_box[0] += 1
                            cp(P, "vector", Lsb[k2][:], ps[:, 6:8, :], reads=[bankR[6], bankR[7]], writes=[Lsb_R[k2]])
                            act(P, O1s[k2][:], ps[:, 4, :], AF.Copy, reads=[bankR[4]], writes=[O1s_R[k2]])
                            act(P, O2s[k2][:], ps[:, 5, :], AF.Copy, reads=[bankR[5]], writes=[O2s_R[k2]])
                            P.op("vector", lambda e: e.reciprocal(out=r12[:], in_=Lsb[k2][:]),
                                 reads=[Lsb_R[k2]], writes=[r12_R])
                            tt(P, "gpsimd", ea[:], O1s[k2][:], r12[:, 0, :], ALU.mult, reads=[O1s_R[k2], r12_R], writes=[ea_R])
                            tt(P, "gpsimd", eb[:], O2s[k2][:], r12[:, 1, :], ALU.mult, reads=[O2s_R[k2], r12_R], writes=[eb_R])
                            stt(P, ed[:], eb[:], neg_lam, ea[:], ALU.mult, ALU.add, reads=[ea_R, eb_R, lam_R], writes=[ed_R])
                            tt(P, "gpsimd", esq[:], ed[:], ed[:], ALU.mult, reads=[ed_R], writes=[esq_R])

                        def epilogue2(h, qb):
                            q0 = qb * 512
                            sbk = 2 * (sp_box[0] % 2)
                            mm(P, ps[:, sbk, :], ones_f[:], esq[:], True, True, reads=[constR, esq_R],
                               writes=[bankR[sbk], bankR[sbk + 1]], mark=True)
                            act(P, eln[:], ps[:, sbk, :], AF.Ln, reads=[bankR[sbk], bankR[sbk + 1]], writes=[eln_R],
                                scale=1.0 / 128.0, bias=EPS)
                            act(P, ers[:], eln[:], AF.Exp, reads=[eln_R], writes=[ers_R], scale=-0.5)
                            stt(P, et[:], ed[:], gs, ers[:], ALU.mult, ALU.mult, reads=[ed_R, ers_R, lam_R], writes=[et_R])
                            yk = (h * 8 + qb) % 2
                            tt(P, "gpsimd", yo[yk][:], et[:], GT[:, q0:q0 + 512], ALU.mult, reads=[et_R, GT_R],
                               writes=[yo_R[yk]])
                            P.dma("sync", yT_v[:, h, q0:q0 + 512], yo[yk][:], reads=[yo_R[yk]], writes=[yT_R[qb]])

                        st = emit_S(*its[0])
                        for i, (qb, kt) in enumerate(its):
                            nkt = 4 * (qb + 1)
                            sb0, c0, jj = st
                            if i + 1 < len(its):
                                st = emit_S(*its[i + 1])
                            pb = Pb[pbi_box[0] % 4]
                            pR = Pb_R[pbi_box[0] % 4]
                            pbi_box[0] += 1
                            first = kt == 0
                            last = kt == nkt - 1
                            for m in range(2):
                                if jj >= -1:
                                    if jj == -1:
                                        a0, wd, bo = 0, 128, 128
                                    else:
                                        a0, wd, bo = 128 * jj, min(256, 512 - 128 * jj), 0
                                    pv = ps[:, sb0 + m, a0:a0 + wd]
                                    tt(P, "vector", pv, pv, BM[:, h, bo:bo + wd], ALU.add, reads=[BM_R],
                                       writes=[bankR[sb0 + m]])
                                act(P, pb[:, m, c0:512], ps[:, sb0 + m, c0:512], AF.Exp,
                                    reads=[bankR[sb0 + m]], writes=[pR[m]])
                            if evq:
                                epilogue1(*evq.pop(0))
                            for m in range(2):
                                mm(P, ps[:, 4 + m, c0:512], Vt[:, kt, :], pb[:, m, c0:512], first, last,
                                   reads=[V_R, pR[m]], writes=[bankR[4 + m]], mark=False)
                                mm(P, ps[:, 6 + m, c0:512], ones_b[:], pb[:, m, c0:512], first, last,
                                   reads=[constR, pR[m]], writes=[bankR[6 + m]], mark=True)
                            tick()
                            if last:
                                evq.append((h, qb))
                                if i + 1 == len(its):
                                    epilogue1(*evq.pop(0))
                                pending.append((tick_box[0] + (6 if qb == 0 else 12), h, qb))
                        if h == NH - 1 or stop:
                            while pending:
                                _, h_, qb_ = pending.pop(0)
                                epilogue2(h_, qb_)
                        if stop == "B1" or stop.startswith("Q"):
                            break
                    P.drain()
                    P.flush()
                if stop:
                    return
            if li_box[0] + 1 < nlayers:
                wes = ExitStack()
                wi_n = wes.enter_context(nc.sbuf_tensor("wiS_L%d" % (li_box[0] + 1), [128, 8, 6144], BF16))
                pre["wi"], pre["wi_R"], pre["es"] = wi_n, R(), wes
            with ExitStack() as pes:
                sbc = lambda name, shape, dt: pes.enter_context(nc.sbuf_tensor(name + "_L%d" % li_box[0], shape, dt))
                wo = sbc("woC", [128, 8, 16, 128], BF16)
                wo_R = [R() for _ in range(8)]
                yt = [sbc("ytC%d" % i, [128, 16, 512], BF16) for i in range(2)]
                yt_R = [R(), R()]
                xt = [sbc("xtC%d" % i, [128, 8, 512], F32) for i in range(2)]
                xt_R = [R(), R()]
                for c in range(8):
                    P.dma("gpsimd", wo[:, c].rearrange("p k m -> p (k m)"), a_wout[j, :, c].rearrange("p k m -> p (k m)"),
                          writes=[wo_R[c]], max_dma_last_dim=2048)
                if "wi" in pre:
                    for c in range(8):
                        for q3 in range(3):
                            P.dma("gpsimd", pre["wi"][:, c, q3 * 2048:(q3 + 1) * 2048],
                                  s_win[j, :, c, q3 * 2048:(q3 + 1) * 2048], writes=[pre["wi_R"]], max_dma_last_dim=2048)

                def c_load(tb):
                    k = tb % 2
                    tsl = slice(tb * 512, (tb + 1) * 512)
                    rd = [] if src_is_x else [xres_R[2 * tb], xres_R[2 * tb + 1]]
                    P.dma("sync", yt[k][:], yT_v[:, :, tsl], reads=[yT_R[tb]], writes=[yt_R[k]])
                    P.dma("sync", xt[k][:], src_v[:, :, tsl], reads=rd, writes=[xt_R[k]])

                c_load(0)
                for tb in range(8):
                    k = tb % 2
                    tsl = slice(tb * 512, (tb + 1) * 512)
                    if tb + 1 < 8:
                        c_load(tb + 1)
                    for c in range(8):
                        b = next_bank(0, 8)
                        for kc in range(16):
                            mm(P, ps[:, b, :], wo[:, c, kc, :], yt[k][:, kc, :], kc == 0, kc == 15,
                               reads=[wo_R[c], yt_R[k]], writes=[bankR[b]], mark=(kc == 15))
                        tt(P, "vector", xt[k][:, c, :], ps[:, b, :], xt[k][:, c, :], ALU.add,
                           reads=[bankR[b]], writes=[xt_R[k]])
                    P.dma("sync", xres_v[:, :, tsl], xt[k][:], reads=[xt_R[k]],
                          writes=[xres_R[2 * tb], xres_R[2 * tb + 1]])
                P.drain()
                P.flush()

        def sgu_layer(j, final):
            NT = 256
            with ExitStack() as pes:
                sb = lambda name, shape, dt: pes.enter_context(nc.sbuf_tensor(name + "_L%d" % li_box[0], shape, dt))
                have_pre = "wi" in pre
                if have_pre:
                    wi, wi_R = pre["wi"], pre["wi_R"]
                else:
                    wi = sb("wiS", [128, 8, 6144], BF16)
                    wi_R = R()
                wo = sb("woS", [128, 16, 1024], BF16)
                wo_R = R()
                gain = sb("gainS", [128, 8], F32)
                gain_R = R()
                vg = sb("vgS", [128, 2048], F32)
                bs = sb("bsS", [128, 2048], F32)
                trl = sb("trlS", [128, 128], F32)
                wsT = sb("wsTS", [128, 16, 128], BF16)
                cst_R = R()
                xt = [sb("xtS%d" % i, [128, 8, NT], F32) for i in range(2)]
                xt_R = [R(), R()]
                hTt = [sb("hTtS%d" % i, [128, 8, NT], BF16) for i in range(2)]
                hTt_R = [R(), R()]
                sq = [sb("sqS%d" % i, [128, NT], F32) for i in range(2)]
                sq_R = [R(), R()]
                lnb = sb("lnbS", [128, NT], F32)
                rstd = sb("rstdS", [128, NT], F32)
                tmpR = [R(), R()]
                vf = sb("vfS", [128, 2048], F32)
                vf_R = R()
                vst = sb("vstS", [128, 4], F32)
                vst_R = R()
                vn = [sb("vnS%d" % i, [128, 2048], BF16) for i in range(2)]
                vn_R = [R(), R()]
                zT = sb("zTS", [128, 16, NT], BF16)
                zT_R = R()
                sg = [sb("sgS%d" % i, [128, NT], F32) for i in range(2)]
                t1 = [sb("t1S%d" % i, [128, NT], F32) for i in range(2)]
                t2 = [sb("t2S%d" % i, [128, NT], F32) for i in range(2)]
                sg_R, t1_R, t2_R = [R(), R()], [R(), R()], [R(), R()]
                wsf = vf[:].rearrange("p (g t) -> p g t", g=16)

                for c in range(8):
                    for q3 in range(3):
                        if have_pre:
                            continue
                        P.dma("gpsimd", wi[:, c, q3 * 2048:(q3 + 1) * 2048], s_win[j, :, c, q3 * 2048:(q3 + 1) * 2048],
                              writes=[wi_R], max_dma_last_dim=2048)
                for q4 in range(4):
                    P.dma("gpsimd", wo[:, q4 * 4:(q4 + 1) * 4, :], s_wout[j, :, q4 * 4:(q4 + 1) * 4, :], writes=[wo_R], max_dma_last_dim=2048)
                P.dma("sync", gain[:], s_norm[j], writes=[gain_R])
                P.dma("sync", vg[:], s_vg[j], writes=[cst_R])
                P.dma("sync", bs[:], s_bs[j], writes=[cst_R])
                P.dma("sync", vf[:], s_ws[j].rearrange("p g t -> p (g t)"), writes=[vf_R])
                P.dma("sync", trl[:], tril, writes=[cst_R])
                tt(P, "vector", wsT[:], wsf, trl[:].unsqueeze(1).broadcast_to([128, 16, 128]), ALU.mult,
                   reads=[cst_R, vf_R], writes=[cst_R])

                NTB = S // NT

                def st_load(tb):
                    k = tb % 2
                    tsl = slice(tb * NT, (tb + 1) * NT)
                    P.dma("sync", xt[k][:], xres_v[:, :, tsl], reads=[xres_R[tb]], writes=[xt_R[k]])

                def st_norm(tb):
                    k = tb % 2
                    norm_fm(xt[k], xt_R[k], NT, gain, gain_R, lambda c, k=k: hTt[k][:, c, :], hTt_R[k],
                            sq, sq_R, lnb, rstd, tmpR, (0, 8))

                def st_v(tb):
                    k = tb % 2
                    for t2i in range(2):
                        for nb in range(4):
                            b = next_bank(0, 8)
                            for c in range(8):
                                mm(P, ps[:, b, :], hTt[k][:, c, t2i * 128:(t2i + 1) * 128],
                                   wi[:, c, 2048 + nb * 512:2048 + (nb + 1) * 512], c == 0, c == 7,
                                   reads=[hTt_R[k], wi_R], writes=[bankR[b]], mark=(c == 7))
                            cp(P, "vector", vf[:, nb * 512:(nb + 1) * 512], ps[:, b, :], reads=[bankR[b]], writes=[vf_R])
                        act(P, vn[t2i][:], vf[:], AF.Square, reads=[vf_R], writes=[vn_R[t2i], vst_R], accum_out=vst[:, 0:1])
                        act(P, vst[:, 1:2], vst[:, 0:1], AF.Ln, reads=[vst_R], writes=[vst_R], scale=1.0 / 2048.0, bias=EPS)
                        act(P, vst[:, 2:3], vst[:, 1:2], AF.Exp, reads=[vst_R], writes=[vst_R], scale=-0.5)
                        stt(P, vn[t2i][:], vf[:], vst[:, 2:3], vg[:], ALU.mult, ALU.mult,
                            reads=[vf_R, vst_R, cst_R], writes=[vn_R[t2i]])

                def st_b(tb):
                    k = tb % 2
                    ug = {}

                    def emit_ug(g):
                        bu = next_bank(0, 8)
                        for c in range(8):
                            mm(P, ps[:, bu, 0:NT], wi[:, c, g * 128:(g + 1) * 128], hTt[k][:, c, :], c == 0, c == 7,
                               reads=[hTt_R[k], wi_R], writes=[bankR[bu]], mark=(c == 7))
                        bg = next_bank(0, 8)
                        for c in range(8):
                            mm(P, ps[:, bg, 0:NT], wi[:, c, 4096 + g * 128:4096 + (g + 1) * 128], hTt[k][:, c, :], c == 0, c == 7,
                               reads=[hTt_R[k], wi_R], writes=[bankR[bg]], mark=(c == 7))
                        kk = g % 2
                        act(P, sg[kk][:], ps[:, bg, 0:NT], AF.Silu, reads=[bankR[bg]], writes=[sg_R[kk]])
                        ug[g] = bu

                    def emit_y(g):
                        kk = g % 2
                        bu = ug.pop(g)
                        by = next_bank(0, 8)
                        for t2i in range(2):
                            mm(P, ps[:, by, t2i * 128:(t2i + 1) * 128], vn[t2i][:, g * 128:(g + 1) * 128], wsT[:, g, :],
                               True, True, reads=[vn_R[t2i], cst_R], writes=[bankR[by]], mark=(t2i == 1))
                        tt(P, "vector", t1[kk][:].rearrange("p (a b) -> p a b", a=2),
                           ps[:, by, 0:NT].rearrange("p (a b) -> p a b", a=2),
                           bs[:, g * 128:(g + 1) * 128].unsqueeze(1).broadcast_to([128, 2, 128]), ALU.add,
                           reads=[bankR[by], cst_R], writes=[t1_R[kk]])
                        tt(P, "vector", t2[kk][:], ps[:, bu, 0:NT], t1[kk][:], ALU.mult,
                           reads=[bankR[bu], t1_R[kk]], writes=[t2_R[kk]])
                        tt(P, "gpsimd", zT[:, g, :], t2[kk][:], sg[kk][:], ALU.mult, reads=[t2_R[kk], sg_R[kk]], writes=[zT_R])

                    LAG = 1
                    for g in range(16 + LAG):
                        if g < 16:
                            emit_ug(g)
                        if g - LAG >= 0:
                            emit_y(g - LAG)

                def st_o(tb):
                    k = tb % 2
                    tsl = slice(tb * NT, (tb + 1) * NT)
                    for c in range(8):
                        b = next_bank(0, 8)
                        for kc in range(16):
                            mm(P, ps[:, b, 0:NT], wo[:, kc, c * 128:(c + 1) * 128], zT[:, kc, :], kc == 0, kc == 15,
                               reads=[wo_R, zT_R], writes=[bankR[b]], mark=(kc == 15))
                        tt(P, "vector", xt[k][:, c, :], ps[:, b, 0:NT], xt[k][:, c, :], ALU.add,
                           reads=[bankR[b]], writes=[xt_R[k]])
                    if final:
                        norm_fm(xt[k], xt_R[k], NT, fnorm_t, constR, lambda c, k=k: xt[k][:, c, :], xt_R[k],
                                sq, sq_R, lnb, rstd, tmpR, (0, 8))
                        P.dma("sync", outT_v[:, :, tsl], xt[k][:], reads=[xt_R[k]], writes=[out_R[tb]])
                    else:
                        P.dma("sync", xres_v[:, :, tsl], xt[k][:], reads=[xt_R[k]], writes=[xres_R[tb]])

                st_load(0)
                st_load(1)
                st_norm(0)
                st_v(0)
                for tb in range(NTB):
                    if tb + 1 < NTB:
                        st_norm(tb + 1)
                    st_b(tb)
                    st_o(tb)
                    if tb + 2 < NTB:
                        st_load(tb + 2)
                    if tb + 1 < NTB:
                        st_v(tb + 1)
                P.drain()
                P.flush()

        for li in range(nlayers):
            j = li // 2
            li_box[0] = li
            if li % 2 == 0:
                attn_layer(j, xT_v if li == 0 else xres_v, li == 0)
            else:
                sgu_layer(j, final=(li == DEPTH - 1))
                if "es" in pre:
                    pre.pop("es").close()
                    pre.clear()
        if nlayers < DEPTH:
            with ExitStack() as pes:
                xt = pes.enter_context(nc.sbuf_tensor("xtD", [128, 8, 512], F32))
                xr = R()
                for tb in range(8):
                    tsl = slice(tb * 512, (tb + 1) * 512)
                    P.dma("sync", xt[:], xres_v[:, :, tsl], reads=[xres_R[2 * tb], xres_R[2 * tb + 1]], writes=[xr])
                    P.dma("sync", outT_v[:, :, tsl], xt[:], reads=[xr], writes=[out_R[tb]])
                P.drain()
                P.flush()
    return nc


def _t5_bucket_np(n):
    n = np.asarray(n, dtype=np.int32)
    max_exact = 16
    nf = np.maximum(n, 1).astype(np.float32)
    large = max_exact + (np.log(nf / np.float32(max_exact)) / np.float32(math.log(128 / max_exact))
                         * np.float32(32 - max_exact)).astype(np.int32)
    large = np.minimum(large, 31)
    return np.where(n < max_exact, n, large)


def make_inputs(inputs):
    f = lambda a: np.ascontiguousarray(np.asarray(a, dtype=np.float32))
    x = f(inputs["x"])
    rel_bias = f(inputs["rel_bias"])
    kl = np.arange(128)[:, None]
    cc = np.arange(256)[None, :]
    n = cc - kl
    bucket = _t5_bucket_np(np.maximum(n, 0))
    G = rel_bias[bucket]
    bmg = f(G.transpose(0, 2, 1).reshape(128, NH * 256))
    rb31 = f(np.broadcast_to(rel_bias[31][None, :], (128, NH)))
    maskc = f(np.where(n >= 0, 0.0, MASKV))
    tril = f((np.arange(128)[:, None] <= np.arange(128)[None, :]).astype(np.float32))

    def pc(v):
        v = f(v)
        return f(v.reshape(v.shape[:-1] + (8, 128)).swapaxes(-1, -2))

    a_win = f(inputs["attn_w_in"]).reshape(2, 8, 128, 4, NH, 128).transpose(0, 4, 2, 1, 3, 5)
    a_win = f(a_win).reshape(2, NH, 128, 8, 512)
    lamv = np.concatenate([f(inputs["attn_lam_q1"]), f(inputs["attn_lam_k1"]),
                           f(inputs["attn_lam_q2"]), f(inputs["attn_lam_k2"])], axis=1)
    a_lam = f(np.broadcast_to(lamv[:, None, :], (2, 128, 256)))
    a_subln = f(inputs["attn_subln"]).reshape(2, 128, 1)
    a_wout = f(f(inputs["attn_w_out"]).reshape(2, 16, 128, 8, 128).transpose(0, 2, 3, 1, 4))
    s_win = f(f(inputs["sgu_w_in"]).reshape(2, 8, 128, 6144).transpose(0, 2, 1, 3))
    s_vg = f(np.broadcast_to(f(inputs["sgu_v_norm"])[:, None, :], (2, 128, 2048)))
    s_ws = f(f(inputs["sgu_w_s"]).transpose(0, 3, 1, 2))
    s_bs = f(np.broadcast_to(f(inputs["sgu_b_s"]).reshape(2, 1, 2048), (2, 128, 2048)))
    s_wout = f(f(inputs["sgu_w_out"]).reshape(2, 16, 128, 1024).transpose(0, 2, 1, 3))
    shared = {
        "bmg": bmg, "rb31": rb31, "maskc": maskc, "tril": tril,
        "a_norm": pc(inputs["attn_norm"]), "a_win": a_win, "a_lam": a_lam, "a_subln": a_subln, "a_wout": a_wout,
        "s_norm": pc(inputs["sgu_norm"]), "s_win": s_win, "s_vg": s_vg, "s_ws": s_ws, "s_bs": s_bs,
        "s_wout": s_wout, "f_norm": pc(inputs["final_norm"]),
    }
    in_maps = []
    for b in range(8):
        m = dict(shared)
        m["xT"] = f(x[b].T)
        in_maps.append(m)
    return in_maps


_NC_CACHE = {}


def kernel(**inputs):
    nl = int(inputs.pop("_nlayers", DEPTH))
    in_maps = make_inputs(inputs)
    if nl not in _NC_CACHE:
        _NC_CACHE[nl] = build_program(nl)
    nc = _NC_CACHE[nl]
    ncores = int(os.environ.get("K_CORES", "8"))
    res = run_bass_kernel_spmd(nc, in_maps[:ncores], core_ids=list(range(ncores)))
    out = np.stack([np.ascontiguousarray(r["outT"].T) for r in res.results], axis=0)
    return out.astype(np.float32)
```
